# Optimizing a Trainium2 kernel written in Bass

```python
import jax, jax.numpy as jnp
from jax import lax
import numpy as np

D_MODEL = 1024
BATCH = 32
SEQ = 2048
DEPTH = 1

CHUNK = 64
EPS = 1e-6

GLA_HEADS = 4
GLA_KEY = D_MODEL // 2
GLA_VAL = D_MODEL
GLA_DK = GLA_KEY // GLA_HEADS
GLA_DV = GLA_VAL // GLA_HEADS
GLA_RANK = 16
GLA_GATE_NORM = 16.0

SSD_INNER = 2 * D_MODEL
SSD_HEADDIM = 64
SSD_HEADS = SSD_INNER // SSD_HEADDIM
SSD_STATE = 128
SSD_GROUPS = 8
SSD_HPG = SSD_HEADS // SSD_GROUPS
SSD_CONV = 4
SSD_CONV_DIM = SSD_INNER + 2 * SSD_GROUPS * SSD_STATE

D_FF = 2816

N_BRANCH = 2
IN_SPLITS = (GLA_KEY, GLA_KEY, GLA_VAL, GLA_VAL, GLA_RANK,
             SSD_INNER, SSD_CONV_DIM, SSD_HEADS, N_BRANCH * D_MODEL)
IN_DIM = sum(IN_SPLITS)

kernel_name = "chunk_causal_gla_ssd_macaron_hybrid"


def rmsnorm(x, w):
    xf = x.astype(jnp.float32)
    y = xf * lax.rsqrt(jnp.mean(xf * xf, axis=-1, keepdims=True) + EPS)
    return (y * w.astype(jnp.float32)).astype(x.dtype)


def swiglu(h, w_gate, w_up, w_down):
    return (jax.nn.silu(h @ w_gate) * (h @ w_up)) @ w_down


def to_scan(t):
    return jnp.swapaxes(t, 0, 1)


def causal_dwconv(x, w, b):
    seq = x.shape[1]
    xp = jnp.pad(x, ((0, 0), (SSD_CONV - 1, 0), (0, 0)))
    y = xp[:, 0:seq] * w[0]
    for i in range(1, SSD_CONV):
        y = y + xp[:, i:i + seq] * w[i]
    return y + b


def gla_mixer(q, k, v, g, f_low, w_f_up, b_f, norm_w, w_o):
    bsz, seq, _ = q.shape
    nc = seq // CHUNK
    log_a = jax.nn.log_sigmoid((f_low @ w_f_up + b_f).astype(jnp.float32)) / GLA_GATE_NORM
    log_a = log_a.reshape(bsz, nc, CHUNK, GLA_HEADS, GLA_DK)
    cum = jnp.cumsum(log_a, axis=2)
    end = cum[:, :, -1]
    qh = (q * GLA_DK ** -0.5).reshape(bsz, nc, CHUNK, GLA_HEADS, GLA_DK)
    k_dec = k.reshape(bsz, nc, CHUNK, GLA_HEADS, GLA_DK) * jnp.exp(end[:, :, None] - cum).astype(k.dtype)
    vh = v.reshape(bsz, nc, CHUNK, GLA_HEADS, GLA_DV)
    decay = jnp.exp(end)

    def step(state, inp):
        qc, kc, vc, dc = inp
        state = state * dc[..., None]
        inter = jnp.einsum('bihd,bhde->bihe', qc, state)
        scores = jnp.einsum('bihd,bjhd->bhij', qc, kc)
        intra = jnp.einsum('bhij,bjhe->bihe', scores, vc)
        state = (state + jnp.einsum('bjhd,bjhe->bhde', kc, vc)).astype(jnp.float32)
        return state, inter + intra

    s0 = jnp.zeros((bsz, GLA_HEADS, GLA_DK, GLA_DV), jnp.float32)
    _, o = lax.scan(step, s0, (to_scan(qh), to_scan(k_dec), to_scan(vh), to_scan(decay)))
    o = jnp.swapaxes(o, 0, 1).reshape(bsz, seq, GLA_HEADS, GLA_DV)
    o = rmsnorm(o, norm_w) * jax.nn.silu(g.reshape(bsz, seq, GLA_HEADS, GLA_DV).astype(jnp.float32))
    return o.reshape(bsz, seq, GLA_VAL).astype(q.dtype) @ w_o


def ssd_mixer(z, xbc, dt_raw, conv_w, conv_b, dt_bias, a_log, d_skip, norm_w, w_o):
    bsz, seq, _ = z.shape
    nc = seq // CHUNK
    xbc = jax.nn.silu(causal_dwconv(xbc, conv_w, conv_b))
    xs, bm, cm = jnp.split(xbc, [SSD_INNER, SSD_INNER + SSD_GROUPS * SSD_STATE], axis=-1)
    dt = jax.nn.softplus(dt_raw.astype(jnp.float32) + dt_bias.astype(jnp.float32))
    da = dt * (-jnp.exp(a_log.astype(jnp.float32)))
    cum = jnp.cumsum(da.reshape(bsz, nc, CHUNK, SSD_HEADS), axis=2)
    end = cum[:, :, -1]
    wgt = (jnp.exp(end[:, :, None] - cum) * dt.reshape(bsz, nc, CHUNK, SSD_HEADS))
    wgt = wgt.reshape(bsz, nc, CHUNK, SSD_GROUPS, SSD_HPG)
    decay = jnp.exp(end).reshape(bsz, nc, SSD_GROUPS, SSD_HPG)
    xh = xs.reshape(bsz, nc, CHUNK, SSD_GROUPS, SSD_HPG, SSD_HEADDIM)
    bh = bm.reshape(bsz, nc, CHUNK, SSD_GROUPS, SSD_STATE)
    ch = cm.reshape(bsz, nc, CHUNK, SSD_GROUPS, SSD_STATE)

    def step(h, inp):
        xc, bc, cc, wc, dc = inp
        h = h * dc[..., None, None]
        y_inter = jnp.einsum('bign,bgrpn->bigrp', cc, h)
        cb = jnp.einsum('bign,bjgn->bgij', cc, bc)
        y_intra = jnp.einsum('bgij,bjgr,bjgrp->bigrp', cb, wc, xc)
        h = (h + jnp.einsum('bjgr,bjgrp,bjgn->bgrpn', wc, xc, bc)).astype(jnp.float32)
        return h, y_inter + y_intra

    h0 = jnp.zeros((bsz, SSD_GROUPS, SSD_HPG, SSD_HEADDIM, SSD_STATE), jnp.float32)
    _, y = lax.scan(step, h0, (to_scan(xh), to_scan(bh), to_scan(ch), to_scan(wgt), to_scan(decay)))
    y = jnp.swapaxes(y, 0, 1).reshape(bsz, seq, SSD_GROUPS, SSD_HPG, SSD_HEADDIM)
    y = y + d_skip.reshape(SSD_GROUPS, SSD_HPG, 1) * xs.reshape(bsz, seq, SSD_GROUPS, SSD_HPG, SSD_HEADDIM)
    y = y.reshape(bsz, seq, SSD_INNER) * jax.nn.silu(z.astype(jnp.float32))
    y = rmsnorm(y.reshape(bsz, seq, SSD_GROUPS, SSD_INNER // SSD_GROUPS),
                norm_w.reshape(SSD_GROUPS, SSD_INNER // SSD_GROUPS))
    return y.reshape(bsz, seq, SSD_INNER).astype(z.dtype) @ w_o


def setup_inputs(seed: int = 0) -> dict:
    key = jax.random.key(seed)
    ks = jax.random.split(key, 24)
    f32 = jnp.float32

    def nrm(k, shape, scale):
        return jax.random.normal(k, shape, f32) * scale

    def gain(k, shape):
        return 1.0 + 0.02 * jax.random.normal(k, shape, f32)

    L = DEPTH
    u = jax.random.uniform(ks[13], (L, SSD_HEADS), f32)
    dt0 = jnp.exp(u * (np.log(0.1) - np.log(0.001)) + np.log(0.001)).astype(f32)
    dt_bias = dt0 + jnp.log(-jnp.expm1(-dt0))
    return {
        "x": jax.random.normal(ks[0], (BATCH, SEQ, D_MODEL), f32),
        "ffn1_norm": gain(ks[1], (L, D_MODEL)),
        "ffn1_w_gate": nrm(ks[2], (L, D_MODEL, D_FF), D_MODEL ** -0.5),
        "ffn1_w_up": nrm(ks[3], (L, D_MODEL, D_FF), D_MODEL ** -0.5),
        "ffn1_w_down": nrm(ks[4], (L, D_FF, D_MODEL), D_FF ** -0.5),
        "mix_norm": gain(ks[5], (L, D_MODEL)),
        "w_in": nrm(ks[6], (L, D_MODEL, IN_DIM), D_MODEL ** -0.5),
        "gla_w_f_up": nrm(ks[7], (L, GLA_RANK, GLA_KEY), GLA_RANK ** -0.5),
        "gla_b_f": nrm(ks[8], (L, GLA_KEY), 0.02),
        "gla_norm": gain(ks[9], (L, GLA_DV)),
        "gla_w_o": nrm(ks[10], (L, GLA_VAL, D_MODEL), GLA_VAL ** -0.5),
        "ssd_conv_w": nrm(ks[11], (L, SSD_CONV, SSD_CONV_DIM), SSD_CONV ** -0.5),
        "ssd_conv_b": nrm(ks[12], (L, SSD_CONV_DIM), 0.01),
        "ssd_dt_bias": dt_bias,
        "ssd_a_log": jnp.log(jax.random.uniform(ks[14], (L, SSD_HEADS), f32, 1.0, 16.0)),
        "ssd_d": gain(ks[15], (L, SSD_HEADS)),
        "ssd_norm": gain(ks[16], (L, SSD_INNER)),
        "ssd_w_o": nrm(ks[17], (L, SSD_INNER, D_MODEL), SSD_INNER ** -0.5),
        "w_out": nrm(ks[18], (L, D_MODEL, D_MODEL), D_MODEL ** -0.5),
        "ffn2_norm": gain(ks[19], (L, D_MODEL)),
        "ffn2_w_gate": nrm(ks[20], (L, D_MODEL, D_FF), D_MODEL ** -0.5),
        "ffn2_w_up": nrm(ks[21], (L, D_MODEL, D_FF), D_MODEL ** -0.5),
        "ffn2_w_down": nrm(ks[22], (L, D_FF, D_MODEL), D_FF ** -0.5),
        "final_norm": gain(ks[23], (D_MODEL,)),
    }


def reference(x, ffn1_norm, ffn1_w_gate, ffn1_w_up, ffn1_w_down, mix_norm, w_in,
              gla_w_f_up, gla_b_f, gla_norm, gla_w_o, ssd_conv_w, ssd_conv_b,
              ssd_dt_bias, ssd_a_log, ssd_d, ssd_norm, ssd_w_o, w_out,
              ffn2_norm, ffn2_w_gate, ffn2_w_up, ffn2_w_down, final_norm):
    bsz, seq, _ = x.shape
    split_idx = list(np.cumsum(IN_SPLITS)[:-1])
    for l in range(DEPTH):
        h = rmsnorm(x, ffn1_norm[l])
        x = x + 0.5 * swiglu(h, ffn1_w_gate[l], ffn1_w_up[l], ffn1_w_down[l])

        h = rmsnorm(x, mix_norm[l])
        proj = h @ w_in[l]
        gq, gk, gv, gg, gf, sz, sxbc, sdt, gates = jnp.split(proj, split_idx, axis=-1)
        u_a = gla_mixer(gq, gk, gv, gg, gf, gla_w_f_up[l], gla_b_f[l], gla_norm[l], gla_w_o[l])
        u_b = ssd_mixer(sz, sxbc, sdt, ssd_conv_w[l], ssd_conv_b[l], ssd_dt_bias[l],
                        ssd_a_log[l], ssd_d[l], ssd_norm[l], ssd_w_o[l])
        gts = jax.nn.sigmoid(gates.astype(jnp.float32)).reshape(bsz, seq, N_BRANCH, D_MODEL)
        merged = gts[:, :, 0] * u_a + gts[:, :, 1] * u_b
        x = x + merged.astype(x.dtype) @ w_out[l]

        h = rmsnorm(x, ffn2_norm[l])
        x = x + 0.5 * swiglu(h, ffn2_w_gate[l], ffn2_w_up[l], ffn2_w_down[l])
    return rmsnorm(x, final_norm)
```

```python
from contextlib import ExitStack
import numpy as np
import concourse.bass as bass
import concourse.mybir as mybir
from concourse.bass_utils import run_bass_kernel_spmd

F32 = mybir.dt.float32
BF16 = mybir.dt.bfloat16
AF = mybir.ActivationFunctionType
ALU = mybir.AluOpType

NCORES = 8
SEQ = 2048
D = 1024
DFF = 2816
T = 512
NT = T // 128
NCH = T // 64
EPS = 1e-6
IN_DIM = 11312
C_Q, C_K, C_V, C_G, C_F, C_Z, C_XBC, C_DT, C_GATE = 0, 512, 1024, 2048, 3072, 3088, 5136, 9232, 9264
PIECE_ELEMS = 4096
NSLOT = 4

CP = {}
_o = 0
for _n, _w in (("nw1", 8), ("nwm", 8), ("nw2", 8), ("nwf", 8), ("gnw", 2), ("snw", 16),
               ("cw", 128), ("cb", 32), ("dfm", 16), ("dtb", 32), ("alog", 32)):
    CP[_n] = (_o, _w)
    _o += _w
CP_COLS = _o


class Buf:
    __slots__ = ("name", "w", "r", "bank")
    epoch = {}

    def __init__(self, name):
        self.name = name
        self.w = None
        self.r = dict(Buf.epoch)
        self.bank = None


class Sched:
    ENGS = ("pe", "act", "dve", "pool", "sp")
    COMPUTE = ("pe", "act", "dve", "pool")

    def __init__(self, nc, stack, n_dma=24):
        self.nc = nc
        self.sem = {}
        for e in self.COMPUTE:
            self.sem[e] = stack.enter_context(nc.semaphore("s_" + e))
        self.n_dma = n_dma
        self.dma_tot = {}
        self.dma_next = {}
        for q in ("sp", "pool"):
            for k in range(n_dma):
                key = "%s%d" % (q, k)
                self.sem[key] = stack.enter_context(nc.semaphore("s_" + key))
                self.dma_tot[key] = 0
            self.dma_next[q] = 0
        self.ops = {e: [] for e in self.ENGS}
        self.cnt = {e: 0 for e in self.ENGS}
        self.seen = {e: {} for e in self.ENGS}
        self.token_fn = None
        self.on_psum_read = None
        self.arena_dma = {}
        Buf.epoch = {}
        self.phase = ""
        self.pe_labels = []
        self.token_buf = Buf("token")

    def _need(self, e, dep):
        if dep is None:
            return
        k, v = dep
        if e == "pe" and k == "pe":
            return
        if self.seen[e].get(k, 0) < v:
            self.seen[e][k] = v
            self.ops[e].append(("wait", k, v))

    def _deps(self, e, reads, writes):
        for b in reads:
            self._need(e, b.w)
        for b in writes:
            self._need(e, b.w)
            for k, v in b.r.items():
                self._need(e, (k, v))

    @staticmethod
    def _mark(sig, reads, writes):
        k, v = sig
        for b in reads:
            if b.r.get(k, 0) < v:
                b.r[k] = v
        for b in writes:
            b.w = sig
            b.r = {}

    def op(self, e, fn, reads=(), writes=()):
        if e != "pe" and self.on_psum_read is not None:
            for b in reads:
                if b.bank is not None:
                    self.on_psum_read(b.bank)
        self._deps(e, reads, writes)
        self.cnt[e] += 1
        sig = (e, self.cnt[e])
        self.ops[e].append(("op", fn, e, self.phase))
        self._mark(sig, reads, writes)

    def epoch(self, rearm=()):
        snap = {e: self.cnt[e] for e in self.COMPUTE if self.cnt[e]}
        snap.update(self.arena_dma)
        Buf.epoch = snap
        for b in rearm:
            for k, v in snap.items():
                if b.r.get(k, 0) < v:
                    b.r[k] = v

    def dma(self, e, out, in_, reads=(), writes=(), arena=False):
        k = self.dma_next[e]
        self.dma_next[e] = (k + 1) % self.n_dma
        key = "%s%d" % (e, k)
        if self.dma_tot[key]:
            self._need(e, (key, self.dma_tot[key]))
        self._deps(e, reads, writes)
        self.dma_tot[key] += 16
        sig = (key, self.dma_tot[key])
        if arena:
            self.arena_dma[key] = self.dma_tot[key]

        def fn(eng, out=out, in_=in_):
            return eng.dma_start(out=out, in_=in_)
        self.ops[e].append(("dma", fn, key))
        self._mark(sig, reads, writes)

    def wait_all_dma(self, e, only_own=False):
        for key, v in self.dma_tot.items():
            if v and (not only_own or key.startswith(e)):
                self._need(e, (key, v))

    def barrier(self):
        self.wait_all_dma("pool", only_own=True)
        if self.token_fn is not None:
            self.op("pool", self.token_fn, writes=[self.token_buf])
        snap = dict(self.cnt)
        for e in self.COMPUTE:
            for f in self.COMPUTE:
                if f != e and snap[f]:
                    self._need(e, (f, snap[f]))

    def emit(self, block):
        eng_of = {"pe": block.tensor, "act": block.scalar, "dve": block.vector,
                  "pool": block.gpsimd, "sp": block.sync}

        def mk(e):
            def body(eng):
                cnt = [0]

                class Proxy:
                    def matmul(self_, *a, **k):
                        cnt[0] += 1
                        return eng.matmul(*a, **k)

                    def transpose(self_, *a, **k):
                        cnt[0] += 1
                        return eng.transpose(*a, **k)
                proxy = Proxy()
                for it in self.ops[e]:
                    if it[0] == "wait":
                        eng.wait_ge(self.sem[it[1]], it[2])
                    elif it[0] == "op":
                        if e == "pe":
                            n0 = cnt[0]
                            it[1](proxy).then_inc(self.sem[it[2]], 1)
                            self.pe_labels.extend([it[3]] * (cnt[0] - n0))
                        else:
                            it[1](eng).then_inc(self.sem[it[2]], 1)
                    else:
                        it[1](eng).then_inc(self.sem[it[2]], 16)
            return body
        for e in self.ENGS:
            eng_of[e](mk(e))


class Arena:
    def __init__(self, ap, nelem):
        self.ap = ap
        self.nbytes = nelem * 2
        self.off = 0
        self.peak = 0
        self.log = {}

    def reset(self, to=0):
        self.off = to

    def alloc(self, free_shape, dtype, name=None):
        n = 1
        for s in free_shape:
            n *= s
        nb = n * (4 if dtype == F32 else 2)
        off = (self.off + 63) // 64 * 64
        assert off + nb <= self.nbytes, ("arena overflow", off + nb, self.nbytes)
        a = self.ap[:, off // 2:(off + nb) // 2]
        self.off = off + nb
        self.peak = max(self.peak, self.off)
        if name:
            self.log[name] = (off, tuple(free_shape), "f32" if dtype == F32 else "bf16")
        if dtype == F32:
            a = a.bitcast(F32)
        if len(free_shape) == 2:
            a = a.rearrange("p (a b) -> p a b", a=free_shape[0])
        elif len(free_shape) == 3:
            a = a.rearrange("p (a b c) -> p a b c", a=free_shape[0], b=free_shape[1])
        return a


def piece_table():
    P = []

    def ffn(pref):
        for j in range(11):
            P.append((pref + "_gu%d" % j, [(pref + "_wg", j * 256, 256, 8), (pref + "_wu", j * 256, 256, 8)]))
        for d in range(8):
            P.append((pref + "_d%d" % d, [(pref + "_wd", d * 128, 128, 22)]))
    ffn("ffn1")
    P.append(("q", [("w_in", C_Q, 512, 8)]))
    P.append(("k", [("w_in", C_K, 512, 8)]))
    P.append(("v0", [("w_in", C_V, 512, 8)]))
    P.append(("v1", [("w_in", C_V + 512, 512, 8)]))
    P.append(("g0", [("w_in", C_G, 512, 8)]))
    P.append(("g1", [("w_in", C_G + 512, 512, 8)]))
    for it in range(0, 32, 4):
        P.append(("x%d" % (it // 4), [("w_in", C_XBC + (it // 4) * 512, 512, 8)]))
        if it % 8 == 0:
            P.append(("z%d" % (it // 8), [("w_in", C_Z + (it // 8) * 512, 512, 8)]))
    for d in range(8):
        P.append(("ma%d" % d, [("gla_wo", d * 128, 128, 8), ("w_in", C_GATE + d * 128, 128, 8)]))
    for d in range(8):
        P.append(("mb%d" % d, [("w_in", C_GATE + 1024 + d * 128, 128, 8), ("ssd_wo", d * 128, 128, 16)]))
    P.append(("wo0", [("w_out", 0, 512, 8)]))
    P.append(("wo1", [("w_out", 512, 512, 8)]))
    ffn("ffn2")
    return P


WSHAPES = {"ffn1_wg": (D, DFF), "ffn1_wu": (D, DFF), "ffn1_wd": (DFF, D), "w_in": (D, IN_DIM),
           "gla_wo": (D, D), "ssd_wo": (2 * D, D), "w_out": (D, D),
           "ffn2_wg": (D, DFF), "ffn2_wu": (D, DFF), "ffn2_wd": (DFF, D)}


def build_nc(n_seq=4, stage="full", cfg=None):
    cfg = cfg or {}
    nc = bass.Bass("TRN2", target_bir_lowering=False)
    ntok = n_seq * SEQ
    x_d = nc.dram_tensor("x", [ntok, D], F32, kind="ExternalInput").ap()
    out_d = nc.dram_tensor("out", [ntok, D], F32, kind="ExternalOutput").ap()
    Wd = {n: nc.dram_tensor(n, list(s), F32, kind="ExternalInput").ap() for n, s in WSHAPES.items()}
    wfu17_d = nc.dram_tensor("wfu17", [17, 512], F32, kind="ExternalInput").ap()
    cpack_d = nc.dram_tensor("cpack", [128, CP_COLS], F32, kind="ExternalInput").ap()
    ptab = piece_table()
    pnames = [n for n, _ in ptab]
    pieces = [sg_ for _, sg_ in ptab]
    NP = len(pieces)
    psize = [sum(nk * ncol for (_, _, ncol, nk) in p) for p in pieces]
    assert max(psize) <= PIECE_ELEMS
    wsc = nc.dram_tensor("wsc", [NP, 128, PIECE_ELEMS], BF16).ap()

    st = ExitStack()
    with st:
        S = Sched(nc, st)

        def sb(name, shape, dt):
            return st.enter_context(nc.sbuf_tensor(name, shape, dt))

        xT = sb("xT", [128, 8, T], F32)
        hT = sb("hT", [128, 8, T], BF16)
        og = sb("og", [128, 8, T], BF16)
        Sst = sb("Sst", [128, 1024], F32)
        Sbb = sb("Sbb", [128, 1024], BF16)
        Hst = sb("Hst", [128, 2048], F32)
        Hbb = sb("Hbb", [128, 2048], BF16)
        wslot = [sb("wslot%d" % i, [128, PIECE_ELEMS], BF16) for i in range(NSLOT)]
        identf = sb("identf", [128, 128], F32)
        identb = sb("identb", [128, 128], BF16)
        ones_bf = sb("ones_bf", [128, 128], BF16)
        ones1 = sb("ones1", [1, 128], F32)
        mrev = sb("mrev", [128, 128], F32)
        onesc = sb("onesc", [128, 2, 128], F32)
        ind16 = sb("ind16", [128, 2], F32)
        cpk = sb("cpk", [128, CP_COLS], F32)
        nw32 = sb("nw32", [128, 32], F32)
        gnw16 = sb("gnw16", [128, 2], F32)
        snw16 = sb("snw16", [128, 16], F32)
        a16 = sb("a16", [128, 32], F32)
        wdt = sb("wdt", [128, 8, 32], BF16)
        wf = sb("wf", [128, 8, 16], BF16)
        wfu = sb("wfu", [17, 512], BF16)
        halo = sb("halo", [128, 32, 3], F32)
        flT = sb("flT", [17, T], BF16)
        dt_t = sb("dt_t", [128, 5, NT, 32], F32)
        dtx, dta, dtv, da16, wgt = (dt_t[:, i_] for i_ in range(5))
        dcb_t = sb("dcb_t", [128, 256], F32)
        dcb = dcb_t[:]
        b_dt, b_wgt, b_dcb = Buf("dt"), Buf("wgt"), Buf("dcb")
        ARENA_ELEMS = cfg.get("arena_elems", 55296)
        arena_t = sb("arena", [128, ARENA_ELEMS], BF16)
        A = Arena(arena_t[:], ARENA_ELEMS)
        ps = st.enter_context(nc.psum_tensor("ps", [128, 4096], F32))

        tok_t = sb("tok_t", [128, 8], F32)
        S.token_fn = lambda e: e.memset(tok_t[:], 0.0)

        def cp(name):
            o, w = CP[name]
            return cpk[:, o:o + w]

        b_xT = [Buf("xT%d" % k) for k in range(8)]
        b_hT = [Buf("hT%d" % k) for k in range(8)]
        b_og = [Buf("og%d" % k) for k in range(8)]
        b_S, b_Sb = Buf("S"), Buf("Sb")
        b_H = [Buf("H0"), Buf("H1")]
        b_Hb = [Buf("Hb0"), Buf("Hb1")]
        b_slot = [Buf("slot%d" % i) for i in range(NSLOT)]
        b_const = Buf("const")
        b_halo = Buf("halo")
        b_fl = Buf("flT")
        b_ps = [Buf("ps%d" % i) for i in range(8)]
        b_prep = [[Buf("prep%d_%d" % (i, j)) for j in range(len(p))] for i, p in enumerate(pieces)]

        for i_ in range(8):
            b_ps[i_].bank = i_
        free_q = [0, 1, 2, 3, 4, 5, 6]
        pmode = [False]

        def _psum_release(i):
            if i < 7 and i not in free_q:
                free_q.append(i)
        S.on_psum_read = _psum_release

        def bank(n=1):
            if n == 1:
                cand = [i for i in free_q if i >= 4] if pmode[0] else free_q
                assert cand, "PSUM banks exhausted"
                pick = cand[0]
                free_q.remove(pick)
                return ps[:, pick * 512:(pick + 1) * 512], [b_ps[pick]]
            for i in free_q:
                if i < 6 and (i ^ 1) in free_q:
                    lo = i & ~1
                    free_q.remove(lo)
                    free_q.remove(lo + 1)
                    return ps[:, lo * 512:(lo + 2) * 512], b_ps[lo:lo + 2]
            raise AssertionError("no aligned PSUM bank pair free")
        stat_bank, b_stat = ps[:, 7 * 512:8 * 512], [b_ps[7]]

        A.reset()
        x_in = A.alloc([NT, D], F32)
        ostage = A.alloc([NT, D], F32)
        actT = A.alloc([22, T], BF16)
        sg = [A.alloc([T], F32) for _ in range(2)]
        sq_t = sb("sq_t", [128, 2, T], BF16)
        rstd_t = sb("rstd_t", [128, T], F32)
        sq = [sq_t[:, 0, :], sq_t[:, 1, :]]
        rstd = rstd_t[:]
        ntmp = [A.alloc([T], F32) for _ in range(2)]
        FFN_TOP = A.off
        b_xin, b_ost = [Buf("x_in%d" % i) for i in range(NT)], Buf("ostage")
        b_act = [Buf("act%d" % c) for c in range(22)]
        b_sg = [Buf("sg0"), Buf("sg1")]
        b_sq = [Buf("sq0"), Buf("sq1")]
        b_rstd = Buf("rstd")
        b_ntmp = [Buf("ntmp0"), Buf("ntmp1")]
        ffn_arena_bufs = b_xin + [b_ost] + b_act + b_sg + b_ntmp

        def load_x(tile_idx, tbs=range(NT)):
            for tb in tbs:
                tok0 = tile_idx * T + tb * 128
                S.dma("pool", x_in[:, tb, :], x_d[tok0:tok0 + 128, :], writes=[b_xin[tb]], arena=True)

        load_x(0)

        S.dma("sp", cpk[:], cpack_d[:, :], writes=[b_const])
        S.op("pool", lambda e: e.memset(identf[:], 1.0), writes=[b_const])
        S.op("pool", lambda e: e.affine_select(out=identf[:], in_=identf[:], pattern=[[-1, 128]],
                                               compare_op=ALU.is_equal, fill=0.0, base=0, channel_multiplier=1),
             reads=[b_const], writes=[b_const])
        S.op("dve", lambda e: e.tensor_copy(out=identb[:], in_=identf[:]), reads=[b_const], writes=[b_const])
        S.op("dve", lambda e: e.memset(ones_bf[:], 1.0), writes=[b_const])
        S.op("dve", lambda e: e.memset(ones1[:], 1.0), writes=[b_const])
        S.op("pool", lambda e: e.memset(mrev[:], -1.0 / 16), writes=[b_const])
        S.op("pool", lambda e: e.affine_select(out=mrev[:], in_=mrev[:], pattern=[[-1, 128]],
                                               compare_op=ALU.is_gt, fill=0.0, base=0, channel_multiplier=1),
             reads=[b_const], writes=[b_const])
        S.op("pool", lambda e: e.memset(mrev[64:128, 0:64], 0.0), reads=[b_const], writes=[b_const])
        S.op("pool", lambda e: e.memset(onesc[:], 0.0), reads=[b_const], writes=[b_const])
        S.op("pool", lambda e: e.memset(onesc[0:64, 0, :], -1.0 / 16), reads=[b_const], writes=[b_const])
        S.op("pool", lambda e: e.memset(onesc[64:128, 1, :], -1.0 / 16), reads=[b_const], writes=[b_const])
        S.op("pool", lambda e: e.memset(ind16[:], 0.0), reads=[b_const], writes=[b_const])
        S.op("pool", lambda e: e.memset(ind16[0:64, 0:1], -1.0 / 16), reads=[b_const], writes=[b_const])
        S.op("pool", lambda e: e.memset(ind16[64:128, 1:2], -1.0 / 16), reads=[b_const], writes=[b_const])
        S.op("pool", lambda e: e.memset(flT[:], 1.0), writes=[b_fl])
        o1 = CP["nw1"][0]
        S.op("dve", lambda e: e.tensor_scalar(out=nw32[:], in0=cpk[:, o1:o1 + 32], scalar1=32.0, scalar2=None,
                                              op0=ALU.mult), reads=[b_const], writes=[b_const])
        S.op("dve", lambda e: e.tensor_scalar(out=gnw16[:], in0=cp("gnw"), scalar1=16.0, scalar2=None,
                                              op0=ALU.mult), reads=[b_const], writes=[b_const])
        S.op("dve", lambda e: e.tensor_scalar(out=snw16[:], in0=cp("snw"), scalar1=16.0, scalar2=None,
                                              op0=ALU.mult), reads=[b_const], writes=[b_const])
        S.op("act", lambda e: e.activation(out=a16[:], in_=cp("alog"), func=AF.Exp), reads=[b_const], writes=[b_const])
        S.op("dve", lambda e: e.tensor_scalar(out=a16[:], in0=a16[:], scalar1=16.0, scalar2=None, op0=ALU.mult),
             reads=[b_const], writes=[b_const])
        w_in = Wd["w_in"]
        S.dma("pool", wdt[:], w_in[:, C_DT:C_DT + 32].rearrange("(k p) c -> p k c", p=128), writes=[b_const])
        S.dma("pool", wf[:], w_in[:, C_F:C_F + 16].rearrange("(k p) c -> p k c", p=128), writes=[b_const])
        S.dma("pool", wfu[:], wfu17_d[:, :], writes=[b_const])

        prep_done = [0]

        def prep_upto(n):
            while prep_done[0] < min(n, NP):
                i = prep_done[0]
                prep_done[0] += 1
                off = 0
                for j, (wn, c0, ncol, nk) in enumerate(pieces[i]):
                    src = Wd[wn][:, c0:c0 + ncol].rearrange("(k p) c -> p k c", p=128)
                    dst = wsc[i, :, off:off + nk * ncol].rearrange("p (k c) -> p k c", k=nk)
                    S.dma("pool", dst, src, writes=[b_prep[i][j]])
                    off += nk * ncol
        PREP_AHEAD = cfg.get("prep_ahead", 12)
        prep_upto(PREP_AHEAD)

        n_tiles = cfg.get("max_tiles", n_seq * (SEQ // T))
        if stage == "ffn1":
            used = [i for i, n in enumerate(pnames) if n.startswith("ffn1")]
        else:
            used = list(range(NP))
        seq_pieces = used * n_tiles
        ws = {"issue": 0, "use": 0}

        def w_issue(i):
            p = seq_pieces[i]
            prep_upto(p + 1 + PREP_AHEAD)
            slot = i % NSLOT
            S.dma("sp", wslot[slot][:, 0:psize[p]], wsc[p, :, 0:psize[p]],
                  reads=b_prep[p], writes=[b_slot[slot]])

        def w_acquire(expect, hold=0):
            i = ws["use"]
            ws["use"] += 1
            assert pnames[seq_pieces[i]] == expect, (pnames[seq_pieces[i]], expect)
            while ws["issue"] < min(i + NSLOT - hold, len(seq_pieces)):
                w_issue(ws["issue"])
                ws["issue"] += 1
            slot = i % NSLOT
            return wslot[slot], b_slot[slot]

        def seg(wsl, off, nk, ncol):
            return wsl[:, off:off + nk * ncol].rearrange("p (k c) -> p k c", k=nk)

        def mm_group(out_ap, pairs):
            def fn(e):
                n = len(pairs)
                last = None
                for i, (l, r) in enumerate(pairs):
                    last = e.matmul(out_ap, lhsT=l, rhs=r, start=(i == 0), stop=(i == n - 1))
                return last
            return fn

        def copy_op(eng, out, in_, reads, writes, scale=None):
            if eng == "act":
                if scale is None:
                    S.op("act", lambda e: e.activation(out=out, in_=in_, func=AF.Copy), reads=reads, writes=writes)
                else:
                    S.op("act", lambda e: e.activation(out=out, in_=in_, func=AF.Copy, scale=scale),
                         reads=reads, writes=writes)
            else:
                assert scale is None
                S.op(eng, lambda e: e.tensor_copy(out=out, in_=in_), reads=reads, writes=writes)

        stq = {"n": 0, "pend": None}

        def stats_push(src_ap, src_buf, last=False):
            i = stq["n"]
            stq["n"] += 1
            s_ = sq[i % 2]
            S.op("act", lambda e: e.activation(out=s_, in_=src_ap, func=AF.Square), reads=[src_buf], writes=[b_sq[i % 2]])
            stats_flush()
            stq["pend"] = (s_, i, last)

        def stats_flush():
            if stq["pend"] is None:
                return
            s_, i, last = stq["pend"]
            stq["pend"] = None
            S.op("pe", lambda e: e.matmul(stat_bank, lhsT=ones_bf[:], rhs=s_, start=(i == 0), stop=last),
                 reads=[b_sq[i % 2], b_const], writes=b_stat)

        def stats_finish(nfeat):
            stats_flush()
            stq["n"] = 0
            S.op("act", lambda e: e.activation(out=rstd, in_=stat_bank, func=AF.Ln, bias=float(nfeat * EPS)),
                 reads=b_stat, writes=[b_rstd])
            S.op("act", lambda e: e.activation(out=rstd, in_=rstd, func=AF.Exp, scale=-0.5),
                 reads=[b_rstd], writes=[b_rstd])

        norm_engs = cfg.get("norm_engs", ["dve"])

        def rmsnorm_h(widx):
            S.phase = "norm"
            stats_finish(D)
            for k in range(8):
                eng = norm_engs[k % len(norm_engs)]
                S.op(eng, lambda e, k=k: e.scalar_tensor_tensor(
                    out=hT[:, k, :], in0=xT[:, k, :], scalar=nw32[:, widx * 8 + k:widx * 8 + k + 1],
                    in1=rstd, op0=ALU.mult, op1=ALU.mult),
                    reads=[b_xT[k], b_rstd, b_const], writes=[b_hT[k]])

        def ffn(base, hooks=None):
            S.phase = "ffn_gu"
            for j in range(11):
                if hooks and j in hooks:
                    hooks[j]()
                wsl, bw = w_acquire(base + "_gu%d" % j)
                wg = seg(wsl, 0, 8, 256)
                wu = seg(wsl, 2048, 8, 256)
                for cc in range(2):
                    c = j * 2 + cc
                    ga, gb = bank()
                    ua, ub = bank()
                    S.op("pe", mm_group(ga, [(wg[:, k, cc * 128:(cc + 1) * 128], hT[:, k, :]) for k in range(8)]),
                         reads=[bw] + b_hT, writes=gb)
                    S.op("pe", mm_group(ua, [(wu[:, k, cc * 128:(cc + 1) * 128], hT[:, k, :]) for k in range(8)]),
                         reads=[bw] + b_hT, writes=ub)
                    s_ = sg[c % 2]
                    S.op("act", lambda e, s_=s_, ga=ga: e.activation(out=s_, in_=ga, func=AF.Silu),
                         reads=gb, writes=[b_sg[c % 2]])
                    S.op("dve", lambda e, s_=s_, ua=ua, c=c: e.tensor_tensor(out=actT[:, c, :], in0=s_, in1=ua, op=ALU.mult),
                         reads=[b_sg[c % 2]] + ub, writes=[b_act[c]])
            S.phase = "ffn_d"
            for d in range(8):
                wsl, bw = w_acquire(base + "_d%d" % d)
                wd_ = seg(wsl, 0, 22, 128)
                oa, ob = bank()
                S.op("pe", mm_group(oa, [(wd_[:, k, :], actT[:, k, :]) for k in range(22)]),
                     reads=[bw] + b_act, writes=ob)
                S.op("dve", lambda e, oa=oa, d=d: e.scalar_tensor_tensor(
                    out=xT[:, d, :], in0=oa, scalar=0.5, in1=xT[:, d, :], op0=ALU.mult, op1=ALU.add),
                    reads=ob + [b_xT[d]], writes=[b_xT[d]])
                stats_push(xT[:, d, :], b_xT[d], last=(d == 7))

        def load_transposes():
            S.phase = "ld_tr"
            for k in range(8):
                pb, bb = bank()

                def tr(e, k=k, pb=pb):
                    last = None
                    for tb in range(NT):
                        last = e.transpose(pb[:, tb * 128:(tb + 1) * 128], x_in[:, tb, k * 128:(k + 1) * 128], identf[:])
                    return last
                S.op("pe", tr, reads=b_xin + [b_const], writes=bb)
                copy_op("act" if k % 2 == 0 else "dve", xT[:, k, :], pb, bb, [b_xT[k]])
                stats_push(xT[:, k, :], b_xT[k], last=(k == 7))

        def store_out(tile_idx, normed):
            S.phase = "store"
            if normed:
                stats_finish(D)
            else:
                stats_flush()
                stq["n"] = 0
            for k in range(8):
                if normed:
                    src = ntmp[k % 2]
                    S.op("dve", lambda e, k=k, src=src: e.scalar_tensor_tensor(
                        out=src, in0=xT[:, k, :], scalar=nw32[:, 24 + k:25 + k], in1=rstd,
                        op0=ALU.mult, op1=ALU.mult),
                        reads=[b_xT[k], b_rstd, b_const], writes=[b_ntmp[k % 2]])
                    rb = [b_ntmp[k % 2]]
                else:
                    src = xT[:, k, :]
                    rb = [b_xT[k]]
                pb, bb = bank()

                def tr(e, src=src, pb=pb):
                    last = None
                    for tb in range(NT):
                        last = e.transpose(pb[:, tb * 128:(tb + 1) * 128], src[:, tb * 128:(tb + 1) * 128], identf[:])
                    return last
                S.op("pe", tr, reads=rb + [b_const], writes=bb)
                copy_op("act" if k % 2 == 0 else "dve", ostage[:, :, k * 128:(k + 1) * 128],
                        pb.rearrange("p (tb c) -> p tb c", tb=NT), bb, [b_ost])
            tok0 = tile_idx * T
            S.dma("pool", out_d[tok0:tok0 + T, :].rearrange("(tb p) d -> p tb d", p=128), ostage,
                  reads=[b_ost], arena=True)

        tap_engs = cfg.get("tap_engs", ["act", "dve", "dve", "dve"])

        def mixer(first_tile_of_seq):
            def dt_path():
                S.phase = "ssd_dt"
                pb, bb = bank()

                def dtmm(e, pb=pb):
                    last = None
                    for tb in range(NT):
                        for k in range(8):
                            last = e.matmul(pb[:, tb * 32:(tb + 1) * 32], lhsT=hT[:, k, tb * 128:(tb + 1) * 128], rhs=wdt[:, k, :],
                                            start=(k == 0), stop=(k == 7))
                    return last
                S.op("pe", dtmm, reads=b_hT + [b_const], writes=bb)
                pbv = pb[:, 0:128].rearrange("p (a b) -> p a b", a=NT)
                S.op("dve", lambda e: e.tensor_tensor(out=dtx, in0=pbv, in1=cp("dtb").unsqueeze(1).to_broadcast([128, NT, 32]),
                                                      op=ALU.add), reads=bb + [b_const], writes=[b_dt])
                S.op("act", lambda e: e.activation(out=dta, in_=dtx, func=AF.Abs), reads=[b_dt], writes=[b_dt])
                S.op("act", lambda e: e.activation(out=dta, in_=dta, func=AF.Exp, scale=-1.0), reads=[b_dt], writes=[b_dt])
                S.op("act", lambda e: e.activation(out=dta, in_=dta, func=AF.Ln, bias=1.0), reads=[b_dt], writes=[b_dt])
                S.op("dve", lambda e: e.scalar_tensor_tensor(out=dtv, in0=dtx, scalar=0.0, in1=dta, op0=ALU.max, op1=ALU.add),
                     reads=[b_dt], writes=[b_dt])
                S.op("dve", lambda e: e.tensor_tensor(out=da16, in0=dtv, in1=a16[:].unsqueeze(1).to_broadcast([128, NT, 32]),
                                                      op=ALU.mult), reads=[b_dt, b_const], writes=[b_dt])
                pb2_, bb2_ = bank()

                def rcmm(e):
                    last = None
                    for tb in range(NT):
                        last = e.matmul(pb2_[:, tb * 32:(tb + 1) * 32], lhsT=mrev[:], rhs=da16[:, tb, :], start=True, stop=True)
                    return last
                S.op("pe", rcmm, reads=[b_dt, b_const], writes=bb2_)
                S.op("act", lambda e: e.activation(out=wgt, in_=pb2_[:, 0:128].rearrange("p (a b) -> p a b", a=NT), func=AF.Exp),
                     reads=bb2_, writes=[b_wgt])
                S.op("dve", lambda e: e.tensor_tensor(out=wgt, in0=wgt, in1=dtv, op=ALU.mult), reads=[b_wgt, b_dt], writes=[b_wgt])
                pb3_, bb3_ = bank()

                def dcmm(e):
                    last = None
                    for c in range(NCH):
                        last = e.matmul(pb3_[:, c * 32:(c + 1) * 32], lhsT=onesc[:, c % 2, :], rhs=da16[:, c // 2, :],
                                        start=True, stop=True)
                    return last
                S.op("pe", dcmm, reads=[b_dt, b_const], writes=bb3_)
                S.op("act", lambda e: e.activation(out=dcb, in_=pb3_[:, 0:256], func=AF.Exp), reads=bb3_, writes=[b_dcb])

            S.phase = "gla_proj"
            A.reset(0)
            qT = A.alloc([4, T], BF16)
            k_tok = A.alloc([NT, 512], BF16)
            v_tok = A.alloc([NT, 1024], BF16)
            l_tok = A.alloc([NT, 512], F32)
            etmp = [A.alloc([512], F32) for _ in range(2)]
            kdec = [A.alloc([512], F32) for _ in range(2)]
            o_f = A.alloc([8, T], F32)
            dec = A.alloc([32], F32)
            rstd4 = A.alloc([4, T], F32)
            sgT = A.alloc([8, T], BF16)
            Sq = A.alloc([1024], F32)
            Sbq = A.alloc([1024], BF16)
            sqg = [A.alloc([T], BF16) for _ in range(8)]
            otmp = [A.alloc([T], F32) for _ in range(2)]
            b_q = [Buf("q%d" % i) for i in range(4)]
            b_k = [Buf("k%d" % i) for i in range(NT)]
            b_v = [Buf("v%d" % i) for i in range(NT)]
            b_l = [Buf("l%d" % i) for i in range(NT)]
            b_et = [Buf("et0"), Buf("et1")]
            b_kd = [Buf("kd0"), Buf("kd1")]
            b_o = [Buf("o%d" % i) for i in range(NCH)]
            b_dec = Buf("dec")
            b_r4 = [Buf("r4_%d" % i) for i in range(4)]
            b_sgT = [Buf("sgT%d" % i) for i in range(8)]
            b_sqg = [Buf("sqg%d" % i) for i in range(8)]
            b_ot = [Buf("ot0"), Buf("ot1")]

            pb, bb = bank()
            S.op("pe", mm_group(pb[0:16, :], [(wf[:, k, :], hT[:, k, :]) for k in range(8)]),
                 reads=b_hT + [b_const], writes=bb)
            copy_op("act", flT[0:16, :], pb[0:16, :], bb, [b_fl])
            wsl, bw = w_acquire("q")
            wv = seg(wsl, 0, 8, 512)
            for c in range(4):
                pb, bb = bank()
                S.op("pe", mm_group(pb, [(wv[:, k, c * 128:(c + 1) * 128], hT[:, k, :]) for k in range(8)]),
                     reads=[bw] + b_hT, writes=bb)
                copy_op("act", qT[:, c, :], pb, bb, [b_q[c]], scale=float(128 ** -0.5))
            dt_path()
            S.phase = "gla_proj"
            for tb in range(NT):
                pb, bb = bank()
                S.op("pe", lambda e, pb=pb, tb=tb: e.matmul(pb, lhsT=flT[0:17, tb * 128:(tb + 1) * 128], rhs=wfu[0:17, :],
                                                            start=True, stop=True),
                     reads=[b_fl, b_const], writes=bb)
                et = etmp[tb % 2]
                S.op("act", lambda e, et=et, pb=pb: e.activation(out=et, in_=pb, func=AF.Exp, scale=-1.0),
                     reads=bb, writes=[b_et[tb % 2]])
                S.op("act", lambda e, et=et, tb=tb: e.activation(out=l_tok[:, tb, :], in_=et, func=AF.Ln, bias=1.0),
                     reads=[b_et[tb % 2]], writes=[b_l[tb]])
            wsl, bw = w_acquire("k")
            wv = seg(wsl, 0, 8, 512)
            for tb in range(NT):
                pr, br = bank()
                S.op("pe", lambda e, pr=pr, tb=tb: e.matmul(pr, lhsT=mrev[:], rhs=l_tok[:, tb, :], start=True, stop=True),
                     reads=[b_l[tb], b_const], writes=br)
                kd = kdec[tb % 2]
                S.op("act", lambda e, kd=kd, pr=pr: e.activation(out=kd, in_=pr, func=AF.Exp),
                     reads=br, writes=[b_kd[tb % 2]])
                pb, bb = bank()
                S.op("pe", mm_group(pb, [(hT[:, k, tb * 128:(tb + 1) * 128], wv[:, k, :]) for k in range(8)]),
                     reads=[bw] + b_hT, writes=bb)
                S.op("dve", lambda e, pb=pb, kd=kd, tb=tb: e.tensor_tensor(out=k_tok[:, tb, :], in0=pb, in1=kd, op=ALU.mult),
                     reads=bb + [b_kd[tb % 2]], writes=[b_k[tb]])
            pd, bd = bank()

            def decmm(e):
                last = None
                for h in range(4):
                    for tb in range(NT):
                        col = h * 8 + tb * 2
                        last = e.matmul(pd[:, col:col + 2], lhsT=l_tok[:, tb, h * 128:(h + 1) * 128], rhs=ind16[:],
                                        start=True, stop=True)
                return last
            S.op("pe", decmm, reads=b_l + [b_const], writes=bd)
            S.op("act", lambda e: e.activation(out=dec, in_=pd[:, 0:32], func=AF.Exp), reads=bd, writes=[b_dec])
            for pv in range(2):
                wsl, bw = w_acquire("v%d" % pv)
                wv = seg(wsl, 0, 8, 512)
                for tb in range(NT):
                    pb, bb = bank()
                    S.op("pe", mm_group(pb, [(hT[:, k, tb * 128:(tb + 1) * 128], wv[:, k, :]) for k in range(8)]),
                         reads=[bw] + b_hT, writes=bb)
                    copy_op("act" if (tb + pv) % 2 == 0 else "dve", v_tok[:, tb, pv * 512:(pv + 1) * 512], pb, bb, [b_v[tb]])
            S.phase = "gla_rec"
            if first_tile_of_seq:
                S.op("pool", lambda e: e.memset(Sst[:], 0.0), writes=[b_S])

            def gla_upd(c):
                tb, hf = c // 2, c % 2
                psl = slice(hf * 64, hf * 64 + 64)
                pb2, bb2 = bank(2)

                def fn(e):
                    last = None
                    for h in range(4):
                        last = e.matmul(pb2[:, h * 256:(h + 1) * 256], lhsT=k_tok[psl, tb, h * 128:(h + 1) * 128],
                                        rhs=v_tok[psl, tb, h * 256:(h + 1) * 256], start=True, stop=True)
                    return last
                S.op("pe", fn, reads=[b_k[tb], b_v[tb]], writes=bb2)
                return pb2, bb2

            S_pp = [Sst[:], Sq]
            Sb_pp = [Sbb[:], Sbq]
            b_Spp = [b_S, Buf("Sq")]
            b_Sbpp = [b_Sb, Buf("Sbq")]

            def gla_rest(c, pb2, bb2, gproj=None):
                src, dst = S_pp[c % 2], S_pp[(c + 1) % 2]
                bsrc, bdst = b_Spp[c % 2], b_Spp[(c + 1) % 2]
                Sbc, bSbc = Sb_pp[c % 2], b_Sbpp[c % 2]

                def fn(e):
                    last = None
                    for h in range(4):
                        last = e.scalar_tensor_tensor(out=dst[:, h * 256:(h + 1) * 256], in0=src[:, h * 256:(h + 1) * 256],
                                                      scalar=dec[:, h * 8 + c:h * 8 + c + 1],
                                                      in1=pb2[:, h * 256:(h + 1) * 256], op0=ALU.mult, op1=ALU.add)
                    return last
                S.op("dve", fn, reads=bb2 + [bsrc, b_dec], writes=[bdst])
                copy_op("act", Sbc, dst, [bdst], [bSbc])
                if gproj is not None:
                    gj, gpb, gbb = gproj
                    S.op("act", lambda e: e.activation(out=sgT[:, gj, :], in_=gpb, func=AF.Silu),
                         reads=gbb, writes=[b_sgT[gj]])
                ob, obb = bank()

                def rd(e):
                    last = None
                    for h in range(4):
                        for eh in range(2):
                            j = h * 2 + eh
                            last = e.matmul(ob[:, j * 64:(j + 1) * 64], lhsT=Sbc[:, h * 256 + eh * 128:h * 256 + (eh + 1) * 128],
                                            rhs=qT[:, h, c * 64:(c + 1) * 64], start=True, stop=True)
                    return last
                S.op("pe", rd, reads=[bSbc] + b_q, writes=obb)
                copy_op("act", o_f[:, :, c * 64:(c + 1) * 64], ob.rearrange("p (j t) -> p j t", t=64), obb, [b_o[c]])

            gw = {}

            def g_proj(j):
                if j % 4 == 0:
                    gw["w"] = w_acquire("g%d" % (j // 4))
                wsl, bw = gw["w"]
                wv = seg(wsl, 0, 8, 512)
                cc = j % 4
                pb, bb = bank()
                S.op("pe", mm_group(pb, [(wv[:, k, cc * 128:(cc + 1) * 128], hT[:, k, :]) for k in range(8)]),
                     reads=[bw] + b_hT, writes=bb)
                return pb, bb

            pmode[0] = True
            pend = gla_upd(0)
            for c in range(NCH):
                nxt = gla_upd(c + 1) if c + 1 < NCH else None
                gpb, gbb = g_proj(c)
                gla_rest(c, *pend, gproj=(c, gpb, gbb))
                pend = nxt
            pmode[0] = False
            S.phase = "gla_out"
            gbanks = []
            for h in range(4):
                pb, bb = bank()
                gbanks.append((pb, bb))
                for eh in range(2):
                    s_ = sqg[h * 2 + eh]
                    S.op("act", lambda e, s_=s_, h=h, eh=eh: e.activation(out=s_, in_=o_f[:, h * 2 + eh, :], func=AF.Square),
                         reads=b_o, writes=[b_sqg[h * 2 + eh]])
                    S.op("pe", lambda e, s_=s_, pb=pb, eh=eh: e.matmul(pb, lhsT=ones_bf[:], rhs=s_, start=(eh == 0), stop=(eh == 1)),
                         reads=[b_sqg[h * 2 + eh], b_const], writes=bb)
            for h in range(4):
                pb, bb = gbanks[h]
                S.op("act", lambda e, pb=pb, h=h: e.activation(out=rstd4[:, h, :], in_=pb, func=AF.Ln, bias=float(256 * EPS)),
                     reads=bb, writes=[b_r4[h]])
                S.op("act", lambda e, h=h: e.activation(out=rstd4[:, h, :], in_=rstd4[:, h, :], func=AF.Exp, scale=-0.5),
                     reads=[b_r4[h]], writes=[b_r4[h]])
                for eh in range(2):
                    j = h * 2 + eh
                    ot = otmp[j % 2]
                    S.op("dve", lambda e, ot=ot, j=j, h=h, eh=eh: e.scalar_tensor_tensor(
                        out=ot, in0=o_f[:, j, :], scalar=gnw16[:, eh:eh + 1], in1=rstd4[:, h, :],
                        op0=ALU.mult, op1=ALU.mult), reads=b_o + [b_r4[h], b_const], writes=[b_ot[j % 2]])
                    S.op("dve", lambda e, ot=ot, j=j: e.tensor_tensor(out=og[:, j, :], in0=ot, in1=sgT[:, j, :], op=ALU.mult),
                         reads=[b_ot[j % 2], b_sgT[j]], writes=[b_og[j]])
            if stage == "gla":
                for k in range(8):
                    S.op("dve", lambda e, k=k: e.tensor_copy(out=xT[:, k, :], in_=og[:, k, :]), reads=[b_og[k], b_xT[k]], writes=[b_xT[k]])
                return
            S.epoch()

            A.reset(0)
            xsT = A.alloc([16, T], BF16, "xsT")
            C_T = A.alloc([8, T], BF16, "C_T")
            Y_OFF = (A.off + 63) // 64 * 64
            y_f = A.alloc([16, T], F32, "y_f")
            SSD_KEEP = A.off
            wx_tok = A.alloc([NT, 2048], BF16, "wx_tok")
            B_tok = A.alloc([NT, 1024], BF16, "B_tok")
            PP_OFF = A.off
            NCB = 4
            A.reset(Y_OFF)
            stg = [A.alloc([T + 4], F32) for _ in range(NCB)]
            acc = [A.alloc([T], F32) for _ in range(NCB)]
            cbt = [A.alloc([T], BF16) for _ in range(NCB)]
            assert A.off <= SSD_KEEP
            SZ_OFF = ARENA_ELEMS * 2 - 16 * T * 2
            A.reset(SZ_OFF)
            szT = A.alloc([16, T], BF16)
            b_szT = [Buf("szT%d" % i) for i in range(16)]
            b_xs = [Buf("xs%d" % i) for i in range(16)]
            b_C = [Buf("C%d" % i) for i in range(8)]
            b_wx = [Buf("wx%d" % i) for i in range(16)]
            b_B = [Buf("B%d" % i) for i in range(8)]
            b_stg = [Buf("stg%d" % i) for i in range(NCB)]
            b_stgh = [Buf("stgh%d" % i) for i in range(NCB)]
            b_acc = [Buf("acc%d" % i) for i in range(NCB)]
            b_cbt = [Buf("cbt%d" % i) for i in range(NCB)]
            S.phase = "ssd_conv"
            if first_tile_of_seq:
                S.op("pool", lambda e: e.memset(halo[:], 0.0), writes=[b_halo])
            cwv = cp("cw").rearrange("p (c t) -> p c t", t=4)
            cbv = cp("cb")
            cst = {}

            def conv_dest(ch):
                if ch < 16:
                    return xsT[:, ch, :], b_xs[ch]
                if ch < 24:
                    return cbt[ch % NCB], b_cbt[ch % NCB]
                return C_T[:, ch - 24, :], b_C[ch - 24]

            def st_a(ch):
                if ch % 4 == 0:
                    cst["w"] = w_acquire("x%d" % (ch // 4), hold=(1 if ch % 8 == 4 else 0))
                wsl, bw = cst["w"]
                wv = seg(wsl, 0, 8, 512)
                cc = ch % 4
                pb, bb = bank()
                S.op("pe", mm_group(pb, [(wv[:, k, cc * 128:(cc + 1) * 128], hT[:, k, :]) for k in range(8)]),
                     reads=[bw] + b_hT, writes=bb)
                st_, bs_, bsh_ = stg[ch % NCB], b_stg[ch % NCB], b_stgh[ch % NCB]
                copy_op("act", st_[:, 3:3 + T], pb, bb, [bs_])
                S.op("pool", lambda e: e.tensor_copy(out=st_[:, 0:3], in_=halo[:, ch, :]), reads=[b_halo], writes=[bsh_])
                S.op("pool", lambda e: e.tensor_copy(out=halo[:, ch, :], in_=st_[:, T:T + 3]), reads=[bs_], writes=[b_halo])
                a_, ba_ = acc[ch % NCB], b_acc[ch % NCB]
                S.op("act", lambda e: e.activation(out=a_, in_=st_[:, 0:T], func=AF.Identity,
                                                   scale=cwv[:, ch, 0:1], bias=cbv[:, ch:ch + 1]),
                     reads=[bs_, bsh_, b_const], writes=[ba_])

            def st_b(ch):
                st_, bs_, bsh_ = stg[ch % NCB], b_stg[ch % NCB], b_stgh[ch % NCB]
                a_, ba_ = acc[ch % NCB], b_acc[ch % NCB]
                for tp in range(1, 4):
                    S.op("dve", lambda e, tp=tp: e.scalar_tensor_tensor(
                        out=a_, in0=st_[:, tp:tp + T], scalar=cwv[:, ch, tp:tp + 1], in1=a_, op0=ALU.mult, op1=ALU.add),
                        reads=[bs_, bsh_, ba_, b_const], writes=[ba_])

            def st_c(ch):
                a_, ba_ = acc[ch % NCB], b_acc[ch % NCB]
                dest, bd_ = conv_dest(ch)
                S.op("act", lambda e: e.activation(out=dest, in_=a_, func=AF.Silu), reads=[ba_], writes=[bd_])

            def st_d(ch):
                if ch >= 24:
                    return
                dest, bd_ = conv_dest(ch)
                pt, bt = bank()
                ptb = pt.bitcast(BF16)

                def tr(e):
                    last = None
                    for tb in range(NT):
                        last = e.transpose(ptb[:, tb * 128:(tb + 1) * 128], dest[:, tb * 128:(tb + 1) * 128], identb[:])
                    return last
                S.op("pe", tr, reads=[bd_, b_const], writes=bt)
                cst[("t", ch)] = (ptb, bt)

            def st_e(ch):
                if ch >= 24:
                    return
                ptb, bt = cst.pop(("t", ch))
                if ch < 16:
                    S.op("dve", lambda e: e.tensor_tensor(
                        out=wx_tok[:, :, ch * 128:(ch + 1) * 128].rearrange("p a (h j) -> p a h j", h=2),
                        in0=ptb[:, 0:512].rearrange("p (a h j) -> p a h j", a=NT, h=2),
                        in1=wgt[:, :, ch * 2:ch * 2 + 2].unsqueeze(3).to_broadcast([128, NT, 2, 64]),
                        op=ALU.mult), reads=bt + [b_wgt], writes=[b_wx[ch]])
                else:
                    copy_op("dve", B_tok[:, :, (ch - 16) * 128:(ch - 15) * 128],
                            ptb[:, 0:512].rearrange("p (a c) -> p a c", a=NT), bt, [b_B[ch - 16]])

            zw = {}

            def z_fill(j):
                if j % 4 == 0:
                    zw["w"] = w_acquire("z%d" % (j // 4), hold=1)
                wsl, bw = zw["w"]
                wv = seg(wsl, 0, 8, 512)
                cc = j % 4
                pb, bb = bank()
                S.op("pe", mm_group(pb, [(wv[:, k, cc * 128:(cc + 1) * 128], hT[:, k, :]) for k in range(8)]),
                     reads=[bw] + b_hT, writes=bb)
                S.op("act", lambda e: e.activation(out=szT[:, j, :], in_=pb, func=AF.Silu), reads=bb, writes=[b_szT[j]])

            stages_ = [st_a, st_b, st_c, st_d, st_e]
            for it in range(32 + len(stages_) - 1):
                for si, fn_ in enumerate(stages_):
                    ch = it - si
                    if 0 <= ch < 32:
                        fn_(ch)
                    if si == 0 and it % 2 == 0 and it // 2 < 16:
                        z_fill(it // 2)
            if stage == "ssd_conv":
                return
            S.phase = "ssd_rec"
            S.epoch()
            b_y = [Buf("y%d" % i) for i in range(16)]
            b_ych = [Buf("ych%d" % i) for i in range(NCH)]
            pool_decay = cfg.get("pool_decay", True)
            if first_tile_of_seq:
                S.op("pool", lambda e: e.memset(Hst[:], 0.0), writes=b_H)

            def ssd_upd(c, gh):
                tb, hf = c // 2, c % 2
                psl = slice(hf * 64, hf * 64 + 64)
                pb2, bb2 = bank(2)

                def fn(e):
                    last = None
                    for gg in range(4):
                        g = gh * 4 + gg
                        last = e.matmul(pb2[:, gg * 256:(gg + 1) * 256], lhsT=B_tok[psl, tb, g * 128:(g + 1) * 128],
                                        rhs=wx_tok[psl, tb, g * 256:(g + 1) * 256], start=True, stop=True)
                    return last
                S.op("pe", fn, reads=b_B + b_wx, writes=bb2)
                return pb2, bb2

            A.reset(PP_OFF)
            Hq = A.alloc([2048], F32)
            Hbq = A.alloc([2048], BF16)
            assert A.off <= SZ_OFF
            H_pp = [Hst[:], Hq]
            Hb_pp = [Hbb[:], Hbq]
            b_Hpp = [b_H, [Buf("Hq0"), Buf("Hq1")]]
            b_Hbpp = [b_Hb, [Buf("Hbq0"), Buf("Hbq1")]]

            def ssd_rest(c, gh, pb2, bb2):
                hs = slice(gh * 1024, (gh + 1) * 1024)
                src, dst = H_pp[c % 2][:, hs], H_pp[(c + 1) % 2][:, hs]
                bsrc, bdst = b_Hpp[c % 2][gh], b_Hpp[(c + 1) % 2][gh]
                Hbc, bHbc = Hb_pp[c % 2], b_Hbpp[c % 2][gh]
                src3 = src.rearrange("p (h j) -> p h j", j=64)
                dst3 = dst.rearrange("p (h j) -> p h j", j=64)
                dsl = dcb[:, c * 32 + gh * 16:c * 32 + gh * 16 + 16]
                deng = "pool" if (pool_decay and gh == 1) else "dve"
                S.op(deng, lambda e: e.tensor_tensor(out=dst3, in0=src3, in1=dsl.unsqueeze(2).to_broadcast([128, 16, 64]),
                                                     op=ALU.mult), reads=[bsrc, b_dcb], writes=[bdst])
                S.op("dve", lambda e: e.tensor_tensor(out=dst, in0=dst, in1=pb2, op=ALU.add),
                     reads=[bdst] + bb2, writes=[bdst])
                copy_op("act", Hbc[:, hs], dst, [bdst], [bHbc])
                ob, obb = bank()

                def rd(e):
                    last = None
                    for gg in range(4):
                        g = gh * 4 + gg
                        for eh in range(2):
                            idx = gg * 2 + eh
                            last = e.matmul(ob[:, idx * 64:(idx + 1) * 64],
                                            lhsT=Hbc[:, g * 256 + eh * 128:g * 256 + (eh + 1) * 128],
                                            rhs=C_T[:, g, c * 64:(c + 1) * 64], start=True, stop=True)
                    return last
                S.op("pe", rd, reads=[bHbc] + b_C, writes=obb)
                copy_op("act", y_f[:, gh * 8:(gh + 1) * 8, c * 64:(c + 1) * 64], ob.rearrange("p (j t) -> p j t", t=64),
                        obb, [b_ych[c]])

            order = [(c, gh) for c in range(NCH) for gh in range(2)]
            pmode[0] = True
            pend = ssd_upd(*order[0])
            for i, (c, gh) in enumerate(order):
                nxt = ssd_upd(*order[i + 1]) if i + 1 < len(order) else None
                ssd_rest(c, gh, *pend)
                pend = nxt
            pmode[0] = False
            if stage == "ssd_rec":
                return
            S.epoch()
            S.phase = "ssd_z"
            A.reset(SSD_KEEP)
            sqz = [A.alloc([T], BF16) for _ in range(4)]
            rstd_g = [A.alloc([T], F32) for _ in range(2)]
            s0 = [A.alloc([T], F32) for _ in range(2)]
            s1 = [A.alloc([T], F32) for _ in range(2)]
            m1 = [A.alloc([T], F32) for _ in range(2)]
            m0all = A.alloc([8, T], BF16)
            merged = A.alloc([8, T], BF16)
            assert A.off <= SZ_OFF, (A.off, SZ_OFF)
            b_sqz = [Buf("sqz%d" % i) for i in range(4)]
            b_rg = [Buf("rg0"), Buf("rg1")]
            b_s0 = [Buf("s00"), Buf("s01")]
            b_s1 = [Buf("s10"), Buf("s11")]
            b_m1 = [Buf("m10"), Buf("m11")]
            b_m0 = [Buf("m0_%d" % i) for i in range(8)]
            b_mg = [Buf("mg%d" % i) for i in range(8)]
            dfm = cp("dfm")

            def merge1(d):
                wsa, bwa = w_acquire("ma%d" % d)
                wa = seg(wsa, 0, 16, 128)
                pua, bua = bank()
                S.op("pe", mm_group(pua, [(wa[:, k, :], og[:, k, :]) for k in range(8)]), reads=[bwa] + b_og, writes=bua)
                pg0, bg0 = bank()
                S.op("pe", mm_group(pg0, [(wa[:, 8 + k, :], hT[:, k, :]) for k in range(8)]), reads=[bwa] + b_hT, writes=bg0)
                i2 = d % 2
                S.op("act", lambda e: e.activation(out=s0[i2], in_=pg0, func=AF.Sigmoid), reads=bg0, writes=[b_s0[i2]])
                S.op("dve", lambda e: e.tensor_tensor(out=m0all[:, d, :], in0=s0[i2], in1=pua, op=ALU.mult),
                     reads=[b_s0[i2]] + bua, writes=[b_m0[d]])

            for j in range(16):
                S.op("dve", lambda e, j=j: e.scalar_tensor_tensor(
                    out=y_f[:, j, :], in0=xsT[:, j, :], scalar=dfm[:, j:j + 1], in1=y_f[:, j, :],
                    op0=ALU.mult, op1=ALU.add), reads=[b_xs[j], b_const] + b_ych, writes=[b_y[j]])
                S.op("dve", lambda e, j=j: e.tensor_tensor(out=y_f[:, j, :], in0=y_f[:, j, :], in1=szT[:, j, :], op=ALU.mult),
                     reads=[b_y[j], b_szT[j]], writes=[b_y[j]])
                if j % 2 == 1:
                    merge1(j // 2)
            S.phase = "ssd_norm"
            nb = {}

            def n_sq(g):
                pb, bb = bank()
                nb[g] = (pb, bb)
                for eh in range(2):
                    j = g * 2 + eh
                    s_ = sqz[(g % 2) * 2 + eh]
                    bs_ = b_sqz[(g % 2) * 2 + eh]
                    S.op("act", lambda e, s_=s_, j=j: e.activation(out=s_, in_=y_f[:, j, :], func=AF.Square),
                         reads=[b_y[j]], writes=[bs_])
                    S.op("pe", lambda e, s_=s_, pb=pb, eh=eh: e.matmul(pb, lhsT=ones_bf[:], rhs=s_, start=(eh == 0), stop=(eh == 1)),
                         reads=[bs_, b_const], writes=bb)

            def n_rs(g):
                pb, bb = nb.pop(g)
                rg = rstd_g[g % 2]
                S.op("act", lambda e: e.activation(out=rg, in_=pb, func=AF.Ln, bias=float(256 * EPS)),
                     reads=bb, writes=[b_rg[g % 2]])
                S.op("act", lambda e: e.activation(out=rg, in_=rg, func=AF.Exp, scale=-0.5),
                     reads=[b_rg[g % 2]], writes=[b_rg[g % 2]])

            def n_sc(g):
                rg = rstd_g[g % 2]
                for eh in range(2):
                    j = g * 2 + eh
                    S.op("dve", lambda e, j=j: e.scalar_tensor_tensor(
                        out=xsT[:, j, :], in0=y_f[:, j, :], scalar=snw16[:, j:j + 1], in1=rg,
                        op0=ALU.mult, op1=ALU.mult), reads=[b_y[j], b_rg[g % 2], b_const], writes=[b_xs[j]])

            for it in range(8 + 2):
                if it < 8:
                    n_sq(it)
                if 0 <= it - 1 < 8:
                    n_rs(it - 1)
                if 0 <= it - 2 < 8:
                    n_sc(it - 2)
            yn, b_yn = xsT, b_xs
            if stage == "ssd":
                hsel = cfg.get("ssd_half", 0)
                for k in range(8):
                    S.op("dve", lambda e, k=k: e.tensor_copy(out=xT[:, k, :], in_=yn[:, hsel * 8 + k, :]),
                         reads=[b_yn[hsel * 8 + k], b_xT[k]], writes=[b_xT[k]])
                return
            S.phase = "merge"
            for d in range(8):
                wsb, bwb = w_acquire("mb%d" % d)
                wb = seg(wsb, 0, 24, 128)
                pg1, bg1 = bank()
                S.op("pe", mm_group(pg1, [(wb[:, k, :], hT[:, k, :]) for k in range(8)]), reads=[bwb] + b_hT, writes=bg1)
                pub, bub = bank()
                S.op("pe", mm_group(pub, [(wb[:, 8 + k, :], yn[:, k, :]) for k in range(16)]), reads=[bwb] + b_yn, writes=bub)
                i2 = d % 2
                S.op("act", lambda e, i2=i2, pg1=pg1: e.activation(out=s1[i2], in_=pg1, func=AF.Sigmoid),
                     reads=bg1, writes=[b_s1[i2]])
                S.op("dve", lambda e, i2=i2, pub=pub: e.tensor_tensor(out=m1[i2], in0=s1[i2], in1=pub, op=ALU.mult),
                     reads=[b_s1[i2]] + bub, writes=[b_m1[i2]])
                S.op(cfg.get("merge_add_eng", "dve"), lambda e, i2=i2, d=d: e.tensor_tensor(out=merged[:, d, :], in0=m0all[:, d, :], in1=m1[i2], op=ALU.add),
                     reads=[b_m0[d], b_m1[i2]], writes=[b_mg[d]])
            S.phase = "wout"
            for pw in range(2):
                wsl, bw = w_acquire("wo%d" % pw)
                wv = seg(wsl, 0, 8, 512)
                for cc in range(4):
                    d = pw * 4 + cc
                    pb, bb = bank()
                    S.op("pe", mm_group(pb, [(wv[:, k, cc * 128:(cc + 1) * 128], merged[:, k, :]) for k in range(8)]),
                         reads=[bw] + b_mg, writes=bb)
                    S.op("dve", lambda e, pb=pb, d=d: e.tensor_tensor(out=xT[:, d, :], in0=xT[:, d, :], in1=pb, op=ALU.add),
                         reads=bb + [b_xT[d]], writes=[b_xT[d]])
                    stats_push(xT[:, d, :], b_xT[d], last=(d == 7))

        tiles_per_seq = SEQ // T
        for ti in range(n_tiles):
            load_transposes()
            if ti + 1 < n_tiles and stage == "ffn1":
                load_x(ti + 1)
            rmsnorm_h(0)
            ffn("ffn1")
            if stage == "ffn1":
                store_out(ti, False)
                continue
            rmsnorm_h(1)
            S.epoch()
            mixer(ti % tiles_per_seq == 0)
            S.epoch(rearm=ffn_arena_bufs)
            xhooks = None
            if ti + 1 < n_tiles:
                xhooks = {1 + 2 * tb: (lambda tb=tb, ti=ti: load_x(ti + 1, [tb])) for tb in range(NT)}
                if stage != "full":
                    load_x(ti + 1)
            if stage in ("mix", "gla", "ssd", "ssd_conv", "ssd_rec"):
                store_out(ti, False)
                continue
            rmsnorm_h(2)
            ffn("ffn2", hooks=xhooks)
            store_out(ti, True)
        S.wait_all_dma("pool")
        S.wait_all_dma("sp")
        with nc.Block() as block:
            S.emit(block)
    nc._arena_log = dict(A.log)
    nc._pe_labels = list(S.pe_labels)
    return nc


def make_cpack(inp):
    f = lambda a: np.asarray(a, dtype=np.float32)
    cols = []
    for n in ("ffn1_norm", "mix_norm", "ffn2_norm"):
        cols.append(f(inp[n]).reshape(8, 128).T)
    cols.append(f(inp["final_norm"]).reshape(8, 128).T)
    cols.append(f(inp["gla_norm"]).reshape(2, 128).T)
    cols.append(f(inp["ssd_norm"]).reshape(16, 128).T)
    cw = f(inp["ssd_conv_w"]).reshape(4, 32, 128)
    cols.append(cw.transpose(2, 1, 0).reshape(128, 128))
    cols.append(f(inp["ssd_conv_b"]).reshape(32, 128).T)
    dd = f(inp["ssd_d"]).reshape(32)
    cols.append(np.repeat(dd.reshape(16, 2), 64, axis=1).T)
    cols.append(np.broadcast_to(f(inp["ssd_dt_bias"]).reshape(1, 32), (128, 32)))
    cols.append(np.broadcast_to(f(inp["ssd_a_log"]).reshape(1, 32), (128, 32)))
    out = np.ascontiguousarray(np.concatenate(cols, axis=1), dtype=np.float32)
    assert out.shape == (128, CP_COLS)
    return out


def make_weight_map(inp):
    f = lambda a, s: np.ascontiguousarray(np.asarray(a, dtype=np.float32).reshape(s))
    m = {
        "ffn1_wg": f(inp["ffn1_w_gate"], (D, DFF)), "ffn1_wu": f(inp["ffn1_w_up"], (D, DFF)),
        "ffn1_wd": f(inp["ffn1_w_down"], (DFF, D)), "w_in": f(inp["w_in"], (D, IN_DIM)),
        "gla_wo": f(inp["gla_w_o"], (D, D)), "ssd_wo": f(inp["ssd_w_o"], (2 * D, D)),
        "w_out": f(inp["w_out"], (D, D)),
        "ffn2_wg": f(inp["ffn2_w_gate"], (D, DFF)), "ffn2_wu": f(inp["ffn2_w_up"], (D, DFF)),
        "ffn2_wd": f(inp["ffn2_w_down"], (DFF, D)),
    }
    m["wfu17"] = np.ascontiguousarray(np.concatenate(
        [np.asarray(inp["gla_w_f_up"], np.float32).reshape(16, 512),
         np.asarray(inp["gla_b_f"], np.float32).reshape(1, 512)], axis=0))
    m["cpack"] = make_cpack(inp)
    return m


_NC_CACHE = {}


def kernel(**inputs):
    x = np.asarray(inputs["x"], dtype=np.float32)
    bsz = x.shape[0]
    per = bsz // NCORES
    key = ("full", per)
    if key not in _NC_CACHE:
        _NC_CACHE[key] = build_nc(n_seq=per, stage="full")
    nc = _NC_CACHE[key]
    wm = make_weight_map(inputs)
    in_maps = []
    for c in range(NCORES):
        m = dict(wm)
        m["x"] = np.ascontiguousarray(x[c * per:(c + 1) * per].reshape(per * SEQ, D))
        in_maps.append(m)
    res = run_bass_kernel_spmd(nc, in_maps, core_ids=list(range(NCORES)))
    out = np.concatenate([np.asarray(r["out"], dtype=np.float32).reshape(per, SEQ, D) for r in res.results], axis=0)
    return out
```

```python
from contextlib import ExitStack
import numpy as np
import concourse.bass as bass
import concourse.mybir as mybir
from concourse.bass_utils import run_bass_kernel_spmd

F32 = mybir.dt.float32
BF16 = mybir.dt.bfloat16
AF = mybir.ActivationFunctionType
ALU = mybir.AluOpType

NCORES = 8
SEQ = 2048
D = 1024
DFF = 2816
T = 512
NT = T // 128
NCH = T // 64
EPS = 1e-6
IN_DIM = 11312
C_Q, C_K, C_V, C_G, C_F, C_Z, C_XBC, C_DT, C_GATE = 0, 512, 1024, 2048, 3072, 3088, 5136, 9232, 9264
PIECE_ELEMS = 4096
NSLOT = 4

CP = {}
_o = 0
for _n, _w in (("nw1", 8), ("nwm", 8), ("nw2", 8), ("nwf", 8), ("gnw", 2), ("snw", 16),
               ("cw", 128), ("cb", 32), ("dfm", 16), ("dtb", 32), ("alog", 32)):
    CP[_n] = (_o, _w)
    _o += _w
CP_COLS = _o


class Buf:
    __slots__ = ("name", "w", "r", "bank")
    epoch = {}

    def __init__(self, name):
        self.name = name
        self.w = None
        self.r = dict(Buf.epoch)
        self.bank = None


class Sched:
    ENGS = ("pe", "act", "dve", "pool", "sp")
    COMPUTE = ("pe", "act", "dve", "pool")

    def __init__(self, nc, stack, n_dma=24):
        self.nc = nc
        self.sem = {}
        for e in self.COMPUTE:
            self.sem[e] = stack.enter_context(nc.semaphore("s_" + e))
        self.n_dma = n_dma
        self.dma_tot = {}
        self.dma_next = {}
        for q in ("sp", "pool"):
            for k in range(n_dma):
                key = "%s%d" % (q, k)
                self.sem[key] = stack.enter_context(nc.semaphore("s_" + key))
                self.dma_tot[key] = 0
            self.dma_next[q] = 0
        self.ops = {e: [] for e in self.ENGS}
        self.cnt = {e: 0 for e in self.ENGS}
        self.seen = {e: {} for e in self.ENGS}
        self.token_fn = None
        self.on_psum_read = None
        self.arena_dma = {}
        Buf.epoch = {}
        self.phase = ""
        self.pe_labels = []
        self.token_buf = Buf("token")

    def _need(self, e, dep):
        if dep is None:
            return
        k, v = dep
        if e == "pe" and k == "pe":
            return
        if self.seen[e].get(k, 0) < v:
            self.seen[e][k] = v
            self.ops[e].append(("wait", k, v))

    def _deps(self, e, reads, writes):
        for b in reads:
            self._need(e, b.w)
        for b in writes:
            self._need(e, b.w)
            for k, v in b.r.items():
                self._need(e, (k, v))

    @staticmethod
    def _mark(sig, reads, writes):
        k, v = sig
        for b in reads:
            if b.r.get(k, 0) < v:
                b.r[k] = v
        for b in writes:
            b.w = sig
            b.r = {}

    def op(self, e, fn, reads=(), writes=()):
        if e != "pe" and self.on_psum_read is not None:
            for b in reads:
                if b.bank is not None:
                    self.on_psum_read(b.bank)
        self._deps(e, reads, writes)
        self.cnt[e] += 1
        sig = (e, self.cnt[e])
        self.ops[e].append(("op", fn, e, self.phase))
        self._mark(sig, reads, writes)

    def epoch(self, rearm=()):
        snap = {e: self.cnt[e] for e in self.COMPUTE if self.cnt[e]}
        snap.update(self.arena_dma)
        Buf.epoch = snap
        for b in rearm:
            for k, v in snap.items():
                if b.r.get(k, 0) < v:
                    b.r[k] = v

    def dma(self, e, out, in_, reads=(), writes=(), arena=False):
        k = self.dma_next[e]
        self.dma_next[e] = (k + 1) % self.n_dma
        key = "%s%d" % (e, k)
        if self.dma_tot[key]:
            self._need(e, (key, self.dma_tot[key]))
        self._deps(e, reads, writes)
        self.dma_tot[key] += 16
        sig = (key, self.dma_tot[key])
        if arena:
            self.arena_dma[key] = self.dma_tot[key]

        def fn(eng, out=out, in_=in_):
            return eng.dma_start(out=out, in_=in_)
        self.ops[e].append(("dma", fn, key))
        self._mark(sig, reads, writes)

    def wait_all_dma(self, e, only_own=False):
        for key, v in self.dma_tot.items():
            if v and (not only_own or key.startswith(e)):
                self._need(e, (key, v))

    def barrier(self):
        self.wait_all_dma("pool", only_own=True)
        if self.token_fn is not None:
            self.op("pool", self.token_fn, writes=[self.token_buf])
        snap = dict(self.cnt)
        for e in self.COMPUTE:
            for f in self.COMPUTE:
                if f != e and snap[f]:
                    self._need(e, (f, snap[f]))

    def emit(self, block):
        eng_of = {"pe": block.tensor, "act": block.scalar, "dve": block.vector,
                  "pool": block.gpsimd, "sp": block.sync}

        def mk(e):
            def body(eng):
                cnt = [0]

                class Proxy:
                    def matmul(self_, *a, **k):
                        cnt[0] += 1
                        return eng.matmul(*a, **k)

                    def transpose(self_, *a, **k):
                        cnt[0] += 1
                        return eng.transpose(*a, **k)
                proxy = Proxy()
                for it in self.ops[e]:
                    if it[0] == "wait":
                        eng.wait_ge(self.sem[it[1]], it[2])
                    elif it[0] == "op":
                        if e == "pe":
                            n0 = cnt[0]
                            it[1](proxy).then_inc(self.sem[it[2]], 1)
                            self.pe_labels.extend([it[3]] * (cnt[0] - n0))
                        else:
                            it[1](eng).then_inc(self.sem[it[2]], 1)
                    else:
                        it[1](eng).then_inc(self.sem[it[2]], 16)
            return body
        for e in self.ENGS:
            eng_of[e](mk(e))


class Arena:
    def __init__(self, ap, nelem):
        self.ap = ap
        self.nbytes = nelem * 2
        self.off = 0
        self.peak = 0
        self.log = {}

    def reset(self, to=0):
        self.off = to

    def alloc(self, free_shape, dtype, name=None):
        n = 1
        for s in free_shape:
            n *= s
        nb = n * (4 if dtype == F32 else 2)
        off = (self.off + 63) // 64 * 64
        assert off + nb <= self.nbytes, ("arena overflow", off + nb, self.nbytes)
        a = self.ap[:, off // 2:(off + nb) // 2]
        self.off = off + nb
        self.peak = max(self.peak, self.off)
        if name:
            self.log[name] = (off, tuple(free_shape), "f32" if dtype == F32 else "bf16")
        if dtype == F32:
            a = a.bitcast(F32)
        if len(free_shape) == 2:
            a = a.rearrange("p (a b) -> p a b", a=free_shape[0])
        elif len(free_shape) == 3:
            a = a.rearrange("p (a b c) -> p a b c", a=free_shape[0], b=free_shape[1])
        return a


def piece_table():
    P = []

    def ffn(pref):
        for j in range(11):
            P.append((pref + "_gu%d" % j, [(pref + "_wg", j * 256, 256, 8), (pref + "_wu", j * 256, 256, 8)]))
        for d in range(8):
            P.append((pref + "_d%d" % d, [(pref + "_wd", d * 128, 128, 22)]))
    ffn("ffn1")
    P.append(("q", [("w_in", C_Q, 512, 8)]))
    P.append(("k", [("w_in", C_K, 512, 8)]))
    P.append(("v0", [("w_in", C_V, 512, 8)]))
    P.append(("v1", [("w_in", C_V + 512, 512, 8)]))
    P.append(("g0", [("w_in", C_G, 512, 8)]))
    P.append(("g1", [("w_in", C_G + 512, 512, 8)]))
    for it in range(0, 32, 4):
        P.append(("x%d" % (it // 4), [("w_in", C_XBC + (it // 4) * 512, 512, 8)]))
        if it % 8 == 0:
            P.append(("z%d" % (it // 8), [("w_in", C_Z + (it // 8) * 512, 512, 8)]))
    for d in range(8):
        P.append(("ma%d" % d, [("gla_wo", d * 128, 128, 8), ("w_in", C_GATE + d * 128, 128, 8)]))
    for d in range(8):
        P.append(("mb%d" % d, [("w_in", C_GATE + 1024 + d * 128, 128, 8), ("ssd_wo", d * 128, 128, 16)]))
    P.append(("wo0", [("w_out", 0, 512, 8)]))
    P.append(("wo1", [("w_out", 512, 512, 8)]))
    ffn("ffn2")
    return P


WSHAPES = {"ffn1_wg": (D, DFF), "ffn1_wu": (D, DFF), "ffn1_wd": (DFF, D), "w_in": (D, IN_DIM),
           "gla_wo": (D, D), "ssd_wo": (2 * D, D), "w_out": (D, D),
           "ffn2_wg": (D, DFF), "ffn2_wu": (D, DFF), "ffn2_wd": (DFF, D)}


def build_nc(n_seq=4, stage="full", cfg=None):
    cfg = cfg or {}
    nc = bass.Bass("TRN2", target_bir_lowering=False)
    ntok = n_seq * SEQ
    x_d = nc.dram_tensor("x", [ntok, D], F32, kind="ExternalInput").ap()
    out_d = nc.dram_tensor("out", [ntok, D], F32, kind="ExternalOutput").ap()
    Wd = {n: nc.dram_tensor(n, list(s), F32, kind="ExternalInput").ap() for n, s in WSHAPES.items()}
    wfu17_d = nc.dram_tensor("wfu17", [17, 512], F32, kind="ExternalInput").ap()
    cpack_d = nc.dram_tensor("cpack", [128, CP_COLS], F32, kind="ExternalInput").ap()
    ptab = piece_table()
    pnames = [n for n, _ in ptab]
    pieces = [sg_ for _, sg_ in ptab]
    NP = len(pieces)
    psize = [sum(nk * ncol for (_, _, ncol, nk) in p) for p in pieces]
    assert max(psize) <= PIECE_ELEMS
    wsc = nc.dram_tensor("wsc", [NP, 128, PIECE_ELEMS], BF16).ap()

    st = ExitStack()
    with st:
        S = Sched(nc, st)

        def sb(name, shape, dt):
            return st.enter_context(nc.sbuf_tensor(name, shape, dt))

        xT = sb("xT", [128, 8, T], F32)
        hT = sb("hT", [128, 8, T], BF16)
        og = sb("og", [128, 8, T], BF16)
        Sst = sb("Sst", [128, 1024], F32)
        Sbb = sb("Sbb", [128, 1024], BF16)
        Hst = sb("Hst", [128, 2048], F32)
        Hbb = sb("Hbb", [128, 2048], BF16)
        wslot = [sb("wslot%d" % i, [128, PIECE_ELEMS], BF16) for i in range(NSLOT)]
        identf = sb("identf", [128, 128], F32)
        identb = sb("identb", [128, 128], BF16)
        ones_bf = sb("ones_bf", [128, 128], BF16)
        ones1 = sb("ones1", [1, 128], F32)
        mrev = sb("mrev", [128, 128], F32)
        onesc = sb("onesc", [128, 2, 128], F32)
        ind16 = sb("ind16", [128, 2], F32)
        cpk = sb("cpk", [128, CP_COLS], F32)
        nw32 = sb("nw32", [128, 32], F32)
        gnw16 = sb("gnw16", [128, 2], F32)
        snw16 = sb("snw16", [128, 16], F32)
        a16 = sb("a16", [128, 32], F32)
        wdt = sb("wdt", [128, 8, 32], BF16)
        wf = sb("wf", [128, 8, 16], BF16)
        wfu = sb("wfu", [17, 512], BF16)
        halo = sb("halo", [128, 32, 3], F32)
        flT = sb("flT", [17, T], BF16)
        dt_t = sb("dt_t", [128, 5, NT, 32], F32)
        dtx, dta, dtv, da16, wgt = (dt_t[:, i_] for i_ in range(5))
        dcb_t = sb("dcb_t", [128, 256], F32)
        dcb = dcb_t[:]
        b_dt, b_wgt, b_dcb = Buf("dt"), Buf("wgt"), Buf("dcb")
        ARENA_ELEMS = cfg.get("arena_elems", 55296)
        arena_t = sb("arena", [128, ARENA_ELEMS], BF16)
        A = Arena(arena_t[:], ARENA_ELEMS)
        ps = st.enter_context(nc.psum_tensor("ps", [128, 4096], F32))

        tok_t = sb("tok_t", [128, 8], F32)
        S.token_fn = lambda e: e.memset(tok_t[:], 0.0)
        b_tokd = Buf("tokd")
        S.op("dve", lambda e: e.memset(tok_t[:], 1.0), writes=[b_tokd])

        def cp(name):
            o, w = CP[name]
            return cpk[:, o:o + w]

        b_xT = [Buf("xT%d" % k) for k in range(8)]
        b_hT = [Buf("hT%d" % k) for k in range(8)]
        b_og = [Buf("og%d" % k) for k in range(8)]
        b_S, b_Sb = Buf("S"), Buf("Sb")
        b_H = [Buf("H0"), Buf("H1")]
        b_Hb = [Buf("Hb0"), Buf("Hb1")]
        b_slot = [Buf("slot%d" % i) for i in range(NSLOT)]
        b_const = Buf("const")
        b_halo = Buf("halo")
        b_fl = Buf("flT")
        b_ps = [Buf("ps%d" % i) for i in range(8)]
        b_prep = [[Buf("prep%d_%d" % (i, j)) for j in range(len(p))] for i, p in enumerate(pieces)]

        for i_ in range(8):
            b_ps[i_].bank = i_
        free_q = [0, 1, 2, 3, 4, 5, 6]
        pmode = [False]

        def _psum_release(i):
            if i < 7 and i not in free_q:
                free_q.append(i)
        S.on_psum_read = _psum_release

        def bank(n=1):
            if n == 1:
                cand = [i for i in free_q if i >= 4] if pmode[0] else free_q
                assert cand, "PSUM banks exhausted"
                pick = cand[0]
                free_q.remove(pick)
                return ps[:, pick * 512:(pick + 1) * 512], [b_ps[pick]]
            for i in free_q:
                if i < 6 and (i ^ 1) in free_q:
                    lo = i & ~1
                    free_q.remove(lo)
                    free_q.remove(lo + 1)
                    return ps[:, lo * 512:(lo + 2) * 512], b_ps[lo:lo + 2]
            raise AssertionError("no aligned PSUM bank pair free")
        stat_bank, b_stat = ps[:, 7 * 512:8 * 512], [b_ps[7]]

        A.reset()
        x_in = A.alloc([NT, D], F32)
        ostage = A.alloc([NT, D], F32)
        actT = A.alloc([22, T], BF16)
        sg = [A.alloc([T], F32) for _ in range(2)]
        sq_t = sb("sq_t", [128, 2, T], BF16)
        rstd_t = sb("rstd_t", [128, T], F32)
        sq = [sq_t[:, 0, :], sq_t[:, 1, :]]
        rstd = rstd_t[:]
        ntmp = [A.alloc([T], F32) for _ in range(2)]
        FFN_TOP = A.off
        b_xin, b_ost = [Buf("x_in%d" % i) for i in range(NT)], Buf("ostage")
        b_act = [Buf("act%d" % c) for c in range(22)]
        b_sg = [Buf("sg0"), Buf("sg1")]
        b_sq = [Buf("sq0"), Buf("sq1")]
        b_rstd = Buf("rstd")
        b_ntmp = [Buf("ntmp0"), Buf("ntmp1")]
        ffn_arena_bufs = b_xin + [b_ost] + b_act + b_sg + b_ntmp

        def load_x(tile_idx, tbs=range(NT)):
            for tb in tbs:
                tok0 = tile_idx * T + tb * 128
                S.dma("pool", x_in[:, tb, :], x_d[tok0:tok0 + 128, :], writes=[b_xin[tb]], arena=True)

        load_x(0)

        S.dma("sp", cpk[:], cpack_d[:, :], writes=[b_const])
        S.op("pool", lambda e: e.memset(identf[:], 1.0), writes=[b_const])
        S.op("pool", lambda e: e.affine_select(out=identf[:], in_=identf[:], pattern=[[-1, 128]],
                                               compare_op=ALU.is_equal, fill=0.0, base=0, channel_multiplier=1),
             reads=[b_const], writes=[b_const])
        S.op("dve", lambda e: e.tensor_copy(out=identb[:], in_=identf[:]), reads=[b_const], writes=[b_const])
        S.op("dve", lambda e: e.memset(ones_bf[:], 1.0), writes=[b_const])
        S.op("dve", lambda e: e.memset(ones1[:], 1.0), writes=[b_const])
        S.op("pool", lambda e: e.memset(mrev[:], -1.0 / 16), writes=[b_const])
        S.op("pool", lambda e: e.affine_select(out=mrev[:], in_=mrev[:], pattern=[[-1, 128]],
                                               compare_op=ALU.is_gt, fill=0.0, base=0, channel_multiplier=1),
             reads=[b_const], writes=[b_const])
        S.op("pool", lambda e: e.memset(mrev[64:128, 0:64], 0.0), reads=[b_const], writes=[b_const])
        S.op("pool", lambda e: e.memset(onesc[:], 0.0), reads=[b_const], writes=[b_const])
        S.op("pool", lambda e: e.memset(onesc[0:64, 0, :], -1.0 / 16), reads=[b_const], writes=[b_const])
        S.op("pool", lambda e: e.memset(onesc[64:128, 1, :], -1.0 / 16), reads=[b_const], writes=[b_const])
        S.op("pool", lambda e: e.memset(ind16[:], 0.0), reads=[b_const], writes=[b_const])
        S.op("pool", lambda e: e.memset(ind16[0:64, 0:1], -1.0 / 16), reads=[b_const], writes=[b_const])
        S.op("pool", lambda e: e.memset(ind16[64:128, 1:2], -1.0 / 16), reads=[b_const], writes=[b_const])
        S.op("pool", lambda e: e.memset(flT[:], 1.0), writes=[b_fl])
        o1 = CP["nw1"][0]
        S.op("dve", lambda e: e.tensor_scalar(out=nw32[:], in0=cpk[:, o1:o1 + 32], scalar1=32.0, scalar2=None,
                                              op0=ALU.mult), reads=[b_const], writes=[b_const])
        S.op("dve", lambda e: e.tensor_scalar(out=gnw16[:], in0=cp("gnw"), scalar1=16.0, scalar2=None,
                                              op0=ALU.mult), reads=[b_const], writes=[b_const])
        S.op("dve", lambda e: e.tensor_scalar(out=snw16[:], in0=cp("snw"), scalar1=16.0, scalar2=None,
                                              op0=ALU.mult), reads=[b_const], writes=[b_const])
        S.op("act", lambda e: e.activation(out=a16[:], in_=cp("alog"), func=AF.Exp), reads=[b_const], writes=[b_const])
        S.op("dve", lambda e: e.tensor_scalar(out=a16[:], in0=a16[:], scalar1=16.0, scalar2=None, op0=ALU.mult),
             reads=[b_const], writes=[b_const])
        w_in = Wd["w_in"]
        S.dma("pool", wdt[:], w_in[:, C_DT:C_DT + 32].rearrange("(k p) c -> p k c", p=128), writes=[b_const])
        S.dma("pool", wf[:], w_in[:, C_F:C_F + 16].rearrange("(k p) c -> p k c", p=128), writes=[b_const])
        S.dma("pool", wfu[:], wfu17_d[:, :], writes=[b_const])

        prep_done = [0]

        def prep_upto(n):
            while prep_done[0] < min(n, NP):
                i = prep_done[0]
                prep_done[0] += 1
                off = 0
                for j, (wn, c0, ncol, nk) in enumerate(pieces[i]):
                    src = Wd[wn][:, c0:c0 + ncol].rearrange("(k p) c -> p k c", p=128)
                    dst = wsc[i, :, off:off + nk * ncol].rearrange("p (k c) -> p k c", k=nk)
                    S.dma("pool", dst, src, writes=[b_prep[i][j]])
                    off += nk * ncol
        PREP_AHEAD = cfg.get("prep_ahead", 12)
        prep_upto(PREP_AHEAD)

        n_tiles = cfg.get("max_tiles", n_seq * (SEQ // T))
        if stage == "ffn1":
            used = [i for i, n in enumerate(pnames) if n.startswith("ffn1")]
        else:
            used = list(range(NP))
        seq_pieces = used * n_tiles
        ws = {"issue": 0, "use": 0}

        def w_issue(i):
            p = seq_pieces[i]
            prep_upto(p + 1 + PREP_AHEAD)
            slot = i % NSLOT
            S.dma("sp", wslot[slot][:, 0:psize[p]], wsc[p, :, 0:psize[p]],
                  reads=b_prep[p], writes=[b_slot[slot]])

        def w_acquire(expect, hold=0):
            i = ws["use"]
            ws["use"] += 1
            assert pnames[seq_pieces[i]] == expect, (pnames[seq_pieces[i]], expect)
            while ws["issue"] < min(i + NSLOT - hold, len(seq_pieces)):
                w_issue(ws["issue"])
                ws["issue"] += 1
            slot = i % NSLOT
            return wslot[slot], b_slot[slot]

        def seg(wsl, off, nk, ncol):
            return wsl[:, off:off + nk * ncol].rearrange("p (k c) -> p k c", k=nk)

        def mm_group(out_ap, pairs):
            def fn(e):
                n = len(pairs)
                last = None
                for i, (l, r) in enumerate(pairs):
                    last = e.matmul(out_ap, lhsT=l, rhs=r, start=(i == 0), stop=(i == n - 1))
                return last
            return fn

        def copy_op(eng, out, in_, reads, writes, scale=None):
            if eng == "act":
                if scale is None:
                    S.op("act", lambda e: e.activation(out=out, in_=in_, func=AF.Copy), reads=reads, writes=writes)
                else:
                    S.op("act", lambda e: e.activation(out=out, in_=in_, func=AF.Copy, scale=scale),
                         reads=reads, writes=writes)
            else:
                assert scale is None
                S.op(eng, lambda e: e.tensor_copy(out=out, in_=in_), reads=reads, writes=writes)

        stq = {"n": 0, "pend": None}

        def stats_push(src_ap, src_buf, last=False):
            i = stq["n"]
            stq["n"] += 1
            s_ = sq[i % 2]
            S.op("act", lambda e: e.activation(out=s_, in_=src_ap, func=AF.Square), reads=[src_buf], writes=[b_sq[i % 2]])
            stats_flush()
            stq["pend"] = (s_, i, last)

        def stats_flush():
            if stq["pend"] is None:
                return
            s_, i, last = stq["pend"]
            stq["pend"] = None
            S.op("pe", lambda e: e.matmul(stat_bank, lhsT=ones_bf[:], rhs=s_, start=(i == 0), stop=last),
                 reads=[b_sq[i % 2], b_const], writes=b_stat)

        def stats_finish(nfeat):
            stats_flush()
            stq["n"] = 0
            S.op("act", lambda e: e.activation(out=rstd, in_=stat_bank, func=AF.Ln, bias=float(nfeat * EPS)),
                 reads=b_stat, writes=[b_rstd])
            S.op("act", lambda e: e.activation(out=rstd, in_=rstd, func=AF.Exp, scale=-0.5),
                 reads=[b_rstd], writes=[b_rstd])

        norm_engs = cfg.get("norm_engs", ["dve"])

        def rmsnorm_h(widx):
            S.phase = "norm"
            stats_finish(D)
            for k in range(8):
                eng = norm_engs[k % len(norm_engs)]
                S.op(eng, lambda e, k=k: e.scalar_tensor_tensor(
                    out=hT[:, k, :], in0=xT[:, k, :], scalar=nw32[:, widx * 8 + k:widx * 8 + k + 1],
                    in1=rstd, op0=ALU.mult, op1=ALU.mult),
                    reads=[b_xT[k], b_rstd, b_const], writes=[b_hT[k]])

        def preload_ln():
            S.op("act", lambda e: e.activation(out=tok_t[:, 0:1], in_=tok_t[:, 1:2], func=AF.Ln, bias=1.0),
                 reads=[b_tokd], writes=[b_tokd])

        def ffn(base, hooks=None):
            S.phase = "ffn_gu"
            for j in range(11):
                if hooks and j in hooks:
                    hooks[j]()
                wsl, bw = w_acquire(base + "_gu%d" % j)
                wg = seg(wsl, 0, 8, 256)
                wu = seg(wsl, 2048, 8, 256)
                for cc in range(2):
                    c = j * 2 + cc
                    ga, gb = bank()
                    ua, ub = bank()
                    S.op("pe", mm_group(ga, [(wg[:, k, cc * 128:(cc + 1) * 128], hT[:, k, :]) for k in range(8)]),
                         reads=[bw] + b_hT, writes=gb)
                    S.op("pe", mm_group(ua, [(wu[:, k, cc * 128:(cc + 1) * 128], hT[:, k, :]) for k in range(8)]),
                         reads=[bw] + b_hT, writes=ub)
                    s_ = sg[c % 2]
                    S.op("act", lambda e, s_=s_, ga=ga: e.activation(out=s_, in_=ga, func=AF.Silu),
                         reads=gb, writes=[b_sg[c % 2]])
                    S.op("dve", lambda e, s_=s_, ua=ua, c=c: e.tensor_tensor(out=actT[:, c, :], in0=s_, in1=ua, op=ALU.mult),
                         reads=[b_sg[c % 2]] + ub, writes=[b_act[c]])
            S.phase = "ffn_d"
            preload_ln()
            for d in range(8):
                wsl, bw = w_acquire(base + "_d%d" % d)
                wd_ = seg(wsl, 0, 22, 128)
                oa, ob = bank()
                S.op("pe", mm_group(oa, [(wd_[:, k, :], actT[:, k, :]) for k in range(22)]),
                     reads=[bw] + b_act, writes=ob)
                S.op("dve", lambda e, oa=oa, d=d: e.scalar_tensor_tensor(
                    out=xT[:, d, :], in0=oa, scalar=0.5, in1=xT[:, d, :], op0=ALU.mult, op1=ALU.add),
                    reads=ob + [b_xT[d]], writes=[b_xT[d]])
                stats_push(xT[:, d, :], b_xT[d], last=(d == 7))

        def load_transposes():
            S.phase = "ld_tr"
            for k in range(8):
                pb, bb = bank()

                def tr(e, k=k, pb=pb):
                    last = None
                    for tb in range(NT):
                        last = e.transpose(pb[:, tb * 128:(tb + 1) * 128], x_in[:, tb, k * 128:(k + 1) * 128], identf[:])
                    return last
                S.op("pe", tr, reads=b_xin + [b_const], writes=bb)
                copy_op("act" if k % 2 == 0 else "dve", xT[:, k, :], pb, bb, [b_xT[k]])
                stats_push(xT[:, k, :], b_xT[k], last=(k == 7))

        def store_out(tile_idx, normed):
            S.phase = "store"
            if normed:
                stats_finish(D)
            else:
                stats_flush()
                stq["n"] = 0
            for k in range(8):
                if normed:
                    src = ntmp[k % 2]
                    S.op("dve", lambda e, k=k, src=src: e.scalar_tensor_tensor(
                        out=src, in0=xT[:, k, :], scalar=nw32[:, 24 + k:25 + k], in1=rstd,
                        op0=ALU.mult, op1=ALU.mult),
                        reads=[b_xT[k], b_rstd, b_const], writes=[b_ntmp[k % 2]])
                    rb = [b_ntmp[k % 2]]
                else:
                    src = xT[:, k, :]
                    rb = [b_xT[k]]
                pb, bb = bank()

                def tr(e, src=src, pb=pb):
                    last = None
                    for tb in range(NT):
                        last = e.transpose(pb[:, tb * 128:(tb + 1) * 128], src[:, tb * 128:(tb + 1) * 128], identf[:])
                    return last
                S.op("pe", tr, reads=rb + [b_const], writes=bb)
                copy_op("act" if k % 2 == 0 else "dve", ostage[:, :, k * 128:(k + 1) * 128],
                        pb.rearrange("p (tb c) -> p tb c", tb=NT), bb, [b_ost])
            tok0 = tile_idx * T
            S.dma("pool", out_d[tok0:tok0 + T, :].rearrange("(tb p) d -> p tb d", p=128), ostage,
                  reads=[b_ost], arena=True)

        def store_and_load(tile_idx, do_load):
            S.phase = "store"
            stats_finish(D)
            for k in range(8):
                if do_load:
                    pbl, bbl = bank()

                    def trl(e, k=k, pbl=pbl):
                        last = None
                        for tb in range(NT):
                            last = e.transpose(pbl[:, tb * 128:(tb + 1) * 128], x_in[:, tb, k * 128:(k + 1) * 128], identf[:])
                        return last
                    S.op("pe", trl, reads=b_xin + [b_const], writes=bbl)
                src = ntmp[k % 2]
                S.op("dve", lambda e, k=k, src=src: e.scalar_tensor_tensor(
                    out=src, in0=xT[:, k, :], scalar=nw32[:, 24 + k:25 + k], in1=rstd,
                    op0=ALU.mult, op1=ALU.mult),
                    reads=[b_xT[k], b_rstd, b_const], writes=[b_ntmp[k % 2]])
                pb, bb = bank()

                def tr(e, src=src, pb=pb):
                    last = None
                    for tb in range(NT):
                        last = e.transpose(pb[:, tb * 128:(tb + 1) * 128], src[:, tb * 128:(tb + 1) * 128], identf[:])
                    return last
                S.op("pe", tr, reads=[b_ntmp[k % 2], b_const], writes=bb)
                copy_op("act" if k % 2 == 0 else "dve", ostage[:, :, k * 128:(k + 1) * 128],
                        pb.rearrange("p (tb c) -> p tb c", tb=NT), bb, [b_ost])
                if do_load:
                    copy_op("dve" if k % 2 == 0 else "act", xT[:, k, :], pbl, bbl, [b_xT[k]])
                    stats_push(xT[:, k, :], b_xT[k], last=(k == 7))
            tok0 = tile_idx * T
            S.dma("pool", out_d[tok0:tok0 + T, :].rearrange("(tb p) d -> p tb d", p=128), ostage,
                  reads=[b_ost], arena=True)

        tap_engs = cfg.get("tap_engs", ["act", "dve", "dve", "dve"])

        def mixer(first_tile_of_seq):
            def dt_path():
                S.phase = "ssd_dt"
                pb, bb = bank()

                def dtmm(e, pb=pb):
                    last = None
                    for tb in range(NT):
                        for k in range(8):
                            last = e.matmul(pb[:, tb * 32:(tb + 1) * 32], lhsT=hT[:, k, tb * 128:(tb + 1) * 128], rhs=wdt[:, k, :],
                                            start=(k == 0), stop=(k == 7))
                    return last
                S.op("pe", dtmm, reads=b_hT + [b_const], writes=bb)
                pbv = pb[:, 0:128].rearrange("p (a b) -> p a b", a=NT)
                S.op("dve", lambda e: e.tensor_tensor(out=dtx, in0=pbv, in1=cp("dtb").unsqueeze(1).to_broadcast([128, NT, 32]),
                                                      op=ALU.add), reads=bb + [b_const], writes=[b_dt])
                S.op("act", lambda e: e.activation(out=dta, in_=dtx, func=AF.Abs), reads=[b_dt], writes=[b_dt])
                S.op("act", lambda e: e.activation(out=dta, in_=dta, func=AF.Exp, scale=-1.0), reads=[b_dt], writes=[b_dt])
                S.op("act", lambda e: e.activation(out=dta, in_=dta, func=AF.Ln, bias=1.0), reads=[b_dt], writes=[b_dt])
                S.op("dve", lambda e: e.scalar_tensor_tensor(out=dtv, in0=dtx, scalar=0.0, in1=dta, op0=ALU.max, op1=ALU.add),
                     reads=[b_dt], writes=[b_dt])
                S.op("dve", lambda e: e.tensor_tensor(out=da16, in0=dtv, in1=a16[:].unsqueeze(1).to_broadcast([128, NT, 32]),
                                                      op=ALU.mult), reads=[b_dt, b_const], writes=[b_dt])
                pb2_, bb2_ = bank()

                def rcmm(e):
                    last = None
                    for tb in range(NT):
                        last = e.matmul(pb2_[:, tb * 32:(tb + 1) * 32], lhsT=mrev[:], rhs=da16[:, tb, :], start=True, stop=True)
                    return last
                S.op("pe", rcmm, reads=[b_dt, b_const], writes=bb2_)
                S.op("act", lambda e: e.activation(out=wgt, in_=pb2_[:, 0:128].rearrange("p (a b) -> p a b", a=NT), func=AF.Exp),
                     reads=bb2_, writes=[b_wgt])
                S.op("dve", lambda e: e.tensor_tensor(out=wgt, in0=wgt, in1=dtv, op=ALU.mult), reads=[b_wgt, b_dt], writes=[b_wgt])
                pb3_, bb3_ = bank()

                def dcmm(e):
                    last = None
                    for c in range(NCH):
                        last = e.matmul(pb3_[:, c * 32:(c + 1) * 32], lhsT=onesc[:, c % 2, :], rhs=da16[:, c // 2, :],
                                        start=True, stop=True)
                    return last
                S.op("pe", dcmm, reads=[b_dt, b_const], writes=bb3_)
                S.op("act", lambda e: e.activation(out=dcb, in_=pb3_[:, 0:256], func=AF.Exp), reads=bb3_, writes=[b_dcb])

            S.phase = "gla_proj"
            A.reset(0)
            qT = A.alloc([4, T], BF16)
            k_tok = A.alloc([NT, 512], BF16)
            v_tok = A.alloc([NT, 1024], BF16)
            l_tok = A.alloc([NT, 512], F32)
            etmp = [A.alloc([512], F32) for _ in range(2)]
            kdec = [A.alloc([512], F32) for _ in range(2)]
            o_f = A.alloc([8, T], F32)
            dec = A.alloc([32], F32)
            rstd4 = A.alloc([4, T], F32)
            sgT = A.alloc([8, T], BF16)
            Sq = A.alloc([1024], F32)
            Sbq = A.alloc([1024], BF16)
            sqg = [A.alloc([T], BF16) for _ in range(8)]
            otmp = [A.alloc([T], F32) for _ in range(2)]
            b_q = [Buf("q%d" % i) for i in range(4)]
            b_k = [Buf("k%d" % i) for i in range(NT)]
            b_v = [Buf("v%d" % i) for i in range(NT)]
            b_l = [Buf("l%d" % i) for i in range(NT)]
            b_et = [Buf("et0"), Buf("et1")]
            b_kd = [Buf("kd0"), Buf("kd1")]
            b_o = [Buf("o%d" % i) for i in range(NCH)]
            b_dec = Buf("dec")
            b_r4 = [Buf("r4_%d" % i) for i in range(4)]
            b_sgT = [Buf("sgT%d" % i) for i in range(8)]
            b_sqg = [Buf("sqg%d" % i) for i in range(8)]
            b_ot = [Buf("ot0"), Buf("ot1")]

            pb, bb = bank()
            S.op("pe", mm_group(pb[0:16, :], [(wf[:, k, :], hT[:, k, :]) for k in range(8)]),
                 reads=b_hT + [b_const], writes=bb)
            copy_op("act", flT[0:16, :], pb[0:16, :], bb, [b_fl])
            wsl, bw = w_acquire("q")
            wv = seg(wsl, 0, 8, 512)
            for c in range(4):
                pb, bb = bank()
                S.op("pe", mm_group(pb, [(wv[:, k, c * 128:(c + 1) * 128], hT[:, k, :]) for k in range(8)]),
                     reads=[bw] + b_hT, writes=bb)
                copy_op("act", qT[:, c, :], pb, bb, [b_q[c]], scale=float(128 ** -0.5))
            dt_path()
            S.phase = "gla_proj"
            for tb in range(NT):
                pb, bb = bank()
                S.op("pe", lambda e, pb=pb, tb=tb: e.matmul(pb, lhsT=flT[0:17, tb * 128:(tb + 1) * 128], rhs=wfu[0:17, :],
                                                            start=True, stop=True),
                     reads=[b_fl, b_const], writes=bb)
                et = etmp[tb % 2]
                S.op("act", lambda e, et=et, pb=pb: e.activation(out=et, in_=pb, func=AF.Exp, scale=-1.0),
                     reads=bb, writes=[b_et[tb % 2]])
                S.op("act", lambda e, et=et, tb=tb: e.activation(out=l_tok[:, tb, :], in_=et, func=AF.Ln, bias=1.0),
                     reads=[b_et[tb % 2]], writes=[b_l[tb]])
            wsl, bw = w_acquire("k")
            wv = seg(wsl, 0, 8, 512)
            for tb in range(NT):
                pr, br = bank()
                S.op("pe", lambda e, pr=pr, tb=tb: e.matmul(pr, lhsT=mrev[:], rhs=l_tok[:, tb, :], start=True, stop=True),
                     reads=[b_l[tb], b_const], writes=br)
                kd = kdec[tb % 2]
                S.op("act", lambda e, kd=kd, pr=pr: e.activation(out=kd, in_=pr, func=AF.Exp),
                     reads=br, writes=[b_kd[tb % 2]])
                pb, bb = bank()
                S.op("pe", mm_group(pb, [(hT[:, k, tb * 128:(tb + 1) * 128], wv[:, k, :]) for k in range(8)]),
                     reads=[bw] + b_hT, writes=bb)
                S.op("dve", lambda e, pb=pb, kd=kd, tb=tb: e.tensor_tensor(out=k_tok[:, tb, :], in0=pb, in1=kd, op=ALU.mult),
                     reads=bb + [b_kd[tb % 2]], writes=[b_k[tb]])
            pd, bd = bank()

            def decmm(e):
                last = None
                for h in range(4):
                    for tb in range(NT):
                        col = h * 8 + tb * 2
                        last = e.matmul(pd[:, col:col + 2], lhsT=l_tok[:, tb, h * 128:(h + 1) * 128], rhs=ind16[:],
                                        start=True, stop=True)
                return last
            S.op("pe", decmm, reads=b_l + [b_const], writes=bd)
            S.op("act", lambda e: e.activation(out=dec, in_=pd[:, 0:32], func=AF.Exp), reads=bd, writes=[b_dec])
            for pv in range(2):
                wsl, bw = w_acquire("v%d" % pv)
                wv = seg(wsl, 0, 8, 512)
                for tb in range(NT):
                    pb, bb = bank()
                    S.op("pe", mm_group(pb, [(hT[:, k, tb * 128:(tb + 1) * 128], wv[:, k, :]) for k in range(8)]),
                         reads=[bw] + b_hT, writes=bb)
                    copy_op("act" if (tb + pv) % 2 == 0 else "dve", v_tok[:, tb, pv * 512:(pv + 1) * 512], pb, bb, [b_v[tb]])
            S.phase = "gla_rec"
            if first_tile_of_seq:
                S.op("pool", lambda e: e.memset(Sst[:], 0.0), writes=[b_S])

            def gla_upd(c):
                tb, hf = c // 2, c % 2
                psl = slice(hf * 64, hf * 64 + 64)
                pb2, bb2 = bank(2)

                def fn(e):
                    last = None
                    for h in range(4):
                        last = e.matmul(pb2[:, h * 256:(h + 1) * 256], lhsT=k_tok[psl, tb, h * 128:(h + 1) * 128],
                                        rhs=v_tok[psl, tb, h * 256:(h + 1) * 256], start=True, stop=True)
                    return last
                S.op("pe", fn, reads=[b_k[tb], b_v[tb]], writes=bb2)
                return pb2, bb2

            S_pp = [Sst[:], Sq]
            Sb_pp = [Sbb[:], Sbq]
            b_Spp = [b_S, Buf("Sq")]
            b_Sbpp = [b_Sb, Buf("Sbq")]

            def gla_rest(c, pb2, bb2, gproj=None):
                src, dst = S_pp[c % 2], S_pp[(c + 1) % 2]
                bsrc, bdst = b_Spp[c % 2], b_Spp[(c + 1) % 2]
                Sbc, bSbc = Sb_pp[c % 2], b_Sbpp[c % 2]

                def fn(e):
                    last = None
                    for h in range(4):
                        last = e.scalar_tensor_tensor(out=dst[:, h * 256:(h + 1) * 256], in0=src[:, h * 256:(h + 1) * 256],
                                                      scalar=dec[:, h * 8 + c:h * 8 + c + 1],
                                                      in1=pb2[:, h * 256:(h + 1) * 256], op0=ALU.mult, op1=ALU.add)
                    return last
                S.op("dve", fn, reads=bb2 + [bsrc, b_dec], writes=[bdst])
                copy_op("act", Sbc, dst, [bdst], [bSbc])
                if gproj is not None:
                    gj, gpb, gbb = gproj
                    S.op("act", lambda e: e.activation(out=sgT[:, gj, :], in_=gpb, func=AF.Silu),
                         reads=gbb, writes=[b_sgT[gj]])
                ob, obb = bank()

                def rd(e):
                    last = None
                    for h in range(4):
                        for eh in range(2):
                            j = h * 2 + eh
                            last = e.matmul(ob[:, j * 64:(j + 1) * 64], lhsT=Sbc[:, h * 256 + eh * 128:h * 256 + (eh + 1) * 128],
                                            rhs=qT[:, h, c * 64:(c + 1) * 64], start=True, stop=True)
                    return last
                S.op("pe", rd, reads=[bSbc] + b_q, writes=obb)
                copy_op("act", o_f[:, :, c * 64:(c + 1) * 64], ob.rearrange("p (j t) -> p j t", t=64), obb, [b_o[c]])

            gw = {}

            def g_proj(j):
                if j % 4 == 0:
                    gw["w"] = w_acquire("g%d" % (j // 4))
                wsl, bw = gw["w"]
                wv = seg(wsl, 0, 8, 512)
                cc = j % 4
                pb, bb = bank()
                S.op("pe", mm_group(pb, [(wv[:, k, cc * 128:(cc + 1) * 128], hT[:, k, :]) for k in range(8)]),
                     reads=[bw] + b_hT, writes=bb)
                return pb, bb

            pmode[0] = True
            pend = gla_upd(0)
            for c in range(NCH):
                nxt = gla_upd(c + 1) if c + 1 < NCH else None
                gpb, gbb = g_proj(c)
                gla_rest(c, *pend, gproj=(c, gpb, gbb))
                pend = nxt
            pmode[0] = False
            S.phase = "gla_out"
            gbanks = []
            for h in range(4):
                pb, bb = bank()
                gbanks.append((pb, bb))
                for eh in range(2):
                    s_ = sqg[h * 2 + eh]
                    S.op("act", lambda e, s_=s_, h=h, eh=eh: e.activation(out=s_, in_=o_f[:, h * 2 + eh, :], func=AF.Square),
                         reads=b_o, writes=[b_sqg[h * 2 + eh]])
                    S.op("pe", lambda e, s_=s_, pb=pb, eh=eh: e.matmul(pb, lhsT=ones_bf[:], rhs=s_, start=(eh == 0), stop=(eh == 1)),
                         reads=[b_sqg[h * 2 + eh], b_const], writes=bb)
            for h in range(4):
                pb, bb = gbanks[h]
                S.op("act", lambda e, pb=pb, h=h: e.activation(out=rstd4[:, h, :], in_=pb, func=AF.Ln, bias=float(256 * EPS)),
                     reads=bb, writes=[b_r4[h]])
                S.op("act", lambda e, h=h: e.activation(out=rstd4[:, h, :], in_=rstd4[:, h, :], func=AF.Exp, scale=-0.5),
                     reads=[b_r4[h]], writes=[b_r4[h]])
                for eh in range(2):
                    j = h * 2 + eh
                    ot = otmp[j % 2]
                    S.op("dve", lambda e, ot=ot, j=j, h=h, eh=eh: e.scalar_tensor_tensor(
                        out=ot, in0=o_f[:, j, :], scalar=gnw16[:, eh:eh + 1], in1=rstd4[:, h, :],
                        op0=ALU.mult, op1=ALU.mult), reads=b_o + [b_r4[h], b_const], writes=[b_ot[j % 2]])
                    S.op("dve", lambda e, ot=ot, j=j: e.tensor_tensor(out=og[:, j, :], in0=ot, in1=sgT[:, j, :], op=ALU.mult),
                         reads=[b_ot[j % 2], b_sgT[j]], writes=[b_og[j]])
            if stage == "gla":
                for k in range(8):
                    S.op("dve", lambda e, k=k: e.tensor_copy(out=xT[:, k, :], in_=og[:, k, :]), reads=[b_og[k], b_xT[k]], writes=[b_xT[k]])
                return
            S.epoch()

            A.reset(0)
            xsT = A.alloc([16, T], BF16, "xsT")
            C_T = A.alloc([8, T], BF16, "C_T")
            Y_OFF = (A.off + 63) // 64 * 64
            y_f = A.alloc([16, T], F32, "y_f")
            SSD_KEEP = A.off
            wx_tok = A.alloc([NT, 2048], BF16, "wx_tok")
            B_tok = A.alloc([NT, 1024], BF16, "B_tok")
            PP_OFF = A.off
            NCB = 4
            A.reset(Y_OFF)
            stg = [A.alloc([T + 4], F32) for _ in range(NCB)]
            acc = [A.alloc([T], F32) for _ in range(NCB)]
            cbt = [A.alloc([T], BF16) for _ in range(NCB)]
            assert A.off <= SSD_KEEP
            SZ_OFF = ARENA_ELEMS * 2 - 16 * T * 2
            A.reset(SZ_OFF)
            szT = A.alloc([16, T], BF16)
            b_szT = [Buf("szT%d" % i) for i in range(16)]
            b_xs = [Buf("xs%d" % i) for i in range(16)]
            b_C = [Buf("C%d" % i) for i in range(8)]
            b_wx = [Buf("wx%d" % i) for i in range(16)]
            b_B = [Buf("B%d" % i) for i in range(8)]
            b_stg = [Buf("stg%d" % i) for i in range(NCB)]
            b_stgh = [Buf("stgh%d" % i) for i in range(NCB)]
            b_acc = [Buf("acc%d" % i) for i in range(NCB)]
            b_cbt = [Buf("cbt%d" % i) for i in range(NCB)]
            S.phase = "ssd_conv"
            if first_tile_of_seq:
                S.op("pool", lambda e: e.memset(halo[:], 0.0), writes=[b_halo])
            cwv = cp("cw").rearrange("p (c t) -> p c t", t=4)
            cbv = cp("cb")
            cst = {}

            def conv_dest(ch):
                if ch < 16:
                    return xsT[:, ch, :], b_xs[ch]
                if ch < 24:
                    return cbt[ch % NCB], b_cbt[ch % NCB]
                return C_T[:, ch - 24, :], b_C[ch - 24]

            def st_a(ch):
                if ch % 4 == 0:
                    cst["w"] = w_acquire("x%d" % (ch // 4), hold=(1 if ch % 8 == 4 else 0))
                wsl, bw = cst["w"]
                wv = seg(wsl, 0, 8, 512)
                cc = ch % 4
                pb, bb = bank()
                S.op("pe", mm_group(pb, [(wv[:, k, cc * 128:(cc + 1) * 128], hT[:, k, :]) for k in range(8)]),
                     reads=[bw] + b_hT, writes=bb)
                st_, bs_, bsh_ = stg[ch % NCB], b_stg[ch % NCB], b_stgh[ch % NCB]
                copy_op("act", st_[:, 3:3 + T], pb, bb, [bs_])
                S.op("pool", lambda e: e.tensor_copy(out=st_[:, 0:3], in_=halo[:, ch, :]), reads=[b_halo], writes=[bsh_])
                S.op("pool", lambda e: e.tensor_copy(out=halo[:, ch, :], in_=st_[:, T:T + 3]), reads=[bs_], writes=[b_halo])
                a_, ba_ = acc[ch % NCB], b_acc[ch % NCB]
                S.op("act", lambda e: e.activation(out=a_, in_=st_[:, 0:T], func=AF.Identity,
                                                   scale=cwv[:, ch, 0:1], bias=cbv[:, ch:ch + 1]),
                     reads=[bs_, bsh_, b_const], writes=[ba_])

            def st_b(ch):
                st_, bs_, bsh_ = stg[ch % NCB], b_stg[ch % NCB], b_stgh[ch % NCB]
                a_, ba_ = acc[ch % NCB], b_acc[ch % NCB]
                for tp in range(1, 4):
                    S.op("dve", lambda e, tp=tp: e.scalar_tensor_tensor(
                        out=a_, in0=st_[:, tp:tp + T], scalar=cwv[:, ch, tp:tp + 1], in1=a_, op0=ALU.mult, op1=ALU.add),
                        reads=[bs_, bsh_, ba_, b_const], writes=[ba_])

            def st_c(ch):
                a_, ba_ = acc[ch % NCB], b_acc[ch % NCB]
                dest, bd_ = conv_dest(ch)
                S.op("act", lambda e: e.activation(out=dest, in_=a_, func=AF.Silu), reads=[ba_], writes=[bd_])

            def st_d(ch):
                if ch >= 24:
                    return
                dest, bd_ = conv_dest(ch)
                pt, bt = bank()
                ptb = pt.bitcast(BF16)

                def tr(e):
                    last = None
                    for tb in range(NT):
                        last = e.transpose(ptb[:, tb * 128:(tb + 1) * 128], dest[:, tb * 128:(tb + 1) * 128], identb[:])
                    return last
                S.op("pe", tr, reads=[bd_, b_const], writes=bt)
                cst[("t", ch)] = (ptb, bt)

            def st_e(ch):
                if ch >= 24:
                    return
                ptb, bt = cst.pop(("t", ch))
                if ch < 16:
                    S.op("dve", lambda e: e.tensor_tensor(
                        out=wx_tok[:, :, ch * 128:(ch + 1) * 128].rearrange("p a (h j) -> p a h j", h=2),
                        in0=ptb[:, 0:512].rearrange("p (a h j) -> p a h j", a=NT, h=2),
                        in1=wgt[:, :, ch * 2:ch * 2 + 2].unsqueeze(3).to_broadcast([128, NT, 2, 64]),
                        op=ALU.mult), reads=bt + [b_wgt], writes=[b_wx[ch]])
                else:
                    copy_op("dve", B_tok[:, :, (ch - 16) * 128:(ch - 15) * 128],
                            ptb[:, 0:512].rearrange("p (a c) -> p a c", a=NT), bt, [b_B[ch - 16]])

            zw = {}

            def z_fill(j):
                if j % 4 == 0:
                    zw["w"] = w_acquire("z%d" % (j // 4), hold=1)
                wsl, bw = zw["w"]
                wv = seg(wsl, 0, 8, 512)
                cc = j % 4
                pb, bb = bank()
                S.op("pe", mm_group(pb, [(wv[:, k, cc * 128:(cc + 1) * 128], hT[:, k, :]) for k in range(8)]),
                     reads=[bw] + b_hT, writes=bb)
                S.op("act", lambda e: e.activation(out=szT[:, j, :], in_=pb, func=AF.Silu), reads=bb, writes=[b_szT[j]])

            stages_ = [st_a, st_b, st_c, st_d, st_e]
            for it in range(32 + len(stages_) - 1):
                for si, fn_ in enumerate(stages_):
                    ch = it - si
                    if 0 <= ch < 32:
                        fn_(ch)
                    if si == 0 and it % 2 == 0 and it // 2 < 16:
                        z_fill(it // 2)
            if stage == "ssd_conv":
                return
            S.phase = "ssd_rec"
            S.epoch()
            b_y = [Buf("y%d" % i) for i in range(16)]
            b_ych = [Buf("ych%d" % i) for i in range(NCH)]
            pool_decay = cfg.get("pool_decay", True)
            if first_tile_of_seq:
                S.op("pool", lambda e: e.memset(Hst[:], 0.0), writes=b_H)

            def ssd_upd(c, gh):
                tb, hf = c // 2, c % 2
                psl = slice(hf * 64, hf * 64 + 64)
                pb2, bb2 = bank(2)

                def fn(e):
                    last = None
                    for gg in range(4):
                        g = gh * 4 + gg
                        last = e.matmul(pb2[:, gg * 256:(gg + 1) * 256], lhsT=B_tok[psl, tb, g * 128:(g + 1) * 128],
                                        rhs=wx_tok[psl, tb, g * 256:(g + 1) * 256], start=True, stop=True)
                    return last
                S.op("pe", fn, reads=b_B + b_wx, writes=bb2)
                return pb2, bb2

            A.reset(PP_OFF)
            Hq = A.alloc([2048], F32)
            Hbq = A.alloc([2048], BF16)
            assert A.off <= SZ_OFF
            H_pp = [Hst[:], Hq]
            Hb_pp = [Hbb[:], Hbq]
            b_Hpp = [b_H, [Buf("Hq0"), Buf("Hq1")]]
            b_Hbpp = [b_Hb, [Buf("Hbq0"), Buf("Hbq1")]]

            def ssd_rest(c, gh, pb2, bb2):
                hs = slice(gh * 1024, (gh + 1) * 1024)
                src, dst = H_pp[c % 2][:, hs], H_pp[(c + 1) % 2][:, hs]
                bsrc, bdst = b_Hpp[c % 2][gh], b_Hpp[(c + 1) % 2][gh]
                Hbc, bHbc = Hb_pp[c % 2], b_Hbpp[c % 2][gh]
                src3 = src.rearrange("p (h j) -> p h j", j=64)
                dst3 = dst.rearrange("p (h j) -> p h j", j=64)
                dsl = dcb[:, c * 32 + gh * 16:c * 32 + gh * 16 + 16]
                deng = "pool" if (pool_decay and gh == 1) else "dve"
                S.op(deng, lambda e: e.tensor_tensor(out=dst3, in0=src3, in1=dsl.unsqueeze(2).to_broadcast([128, 16, 64]),
                                                     op=ALU.mult), reads=[bsrc, b_dcb], writes=[bdst])
                S.op("dve", lambda e: e.tensor_tensor(out=dst, in0=dst, in1=pb2, op=ALU.add),
                     reads=[bdst] + bb2, writes=[bdst])
                copy_op("act", Hbc[:, hs], dst, [bdst], [bHbc])
                ob, obb = bank()

                def rd(e):
                    last = None
                    for gg in range(4):
                        g = gh * 4 + gg
                        for eh in range(2):
                            idx = gg * 2 + eh
                            last = e.matmul(ob[:, idx * 64:(idx + 1) * 64],
                                            lhsT=Hbc[:, g * 256 + eh * 128:g * 256 + (eh + 1) * 128],
                                            rhs=C_T[:, g, c * 64:(c + 1) * 64], start=True, stop=True)
                    return last
                S.op("pe", rd, reads=[bHbc] + b_C, writes=obb)
                copy_op("act", y_f[:, gh * 8:(gh + 1) * 8, c * 64:(c + 1) * 64], ob.rearrange("p (j t) -> p j t", t=64),
                        obb, [b_ych[c]])

            order = [(c, gh) for c in range(NCH) for gh in range(2)]
            pmode[0] = True
            pend = ssd_upd(*order[0])
            for i, (c, gh) in enumerate(order):
                nxt = ssd_upd(*order[i + 1]) if i + 1 < len(order) else None
                ssd_rest(c, gh, *pend)
                pend = nxt
            pmode[0] = False
            if stage == "ssd_rec":
                return
            S.epoch()
            S.phase = "ssd_z"
            A.reset(SSD_KEEP)
            sqz = [A.alloc([T], BF16) for _ in range(4)]
            rstd_g = [A.alloc([T], F32) for _ in range(2)]
            s0 = [A.alloc([T], F32) for _ in range(2)]
            s1 = [A.alloc([T], F32) for _ in range(2)]
            m1 = [A.alloc([T], F32) for _ in range(2)]
            m0all = A.alloc([8, T], BF16)
            merged = A.alloc([8, T], BF16)
            assert A.off <= SZ_OFF, (A.off, SZ_OFF)
            b_sqz = [Buf("sqz%d" % i) for i in range(4)]
            b_rg = [Buf("rg0"), Buf("rg1")]
            b_s0 = [Buf("s00"), Buf("s01")]
            b_s1 = [Buf("s10"), Buf("s11")]
            b_m1 = [Buf("m10"), Buf("m11")]
            b_m0 = [Buf("m0_%d" % i) for i in range(8)]
            b_mg = [Buf("mg%d" % i) for i in range(8)]
            dfm = cp("dfm")

            def merge1(d):
                wsa, bwa = w_acquire("ma%d" % d)
                wa = seg(wsa, 0, 16, 128)
                pua, bua = bank()
                S.op("pe", mm_group(pua, [(wa[:, k, :], og[:, k, :]) for k in range(8)]), reads=[bwa] + b_og, writes=bua)
                pg0, bg0 = bank()
                S.op("pe", mm_group(pg0, [(wa[:, 8 + k, :], hT[:, k, :]) for k in range(8)]), reads=[bwa] + b_hT, writes=bg0)
                i2 = d % 2
                S.op("act", lambda e: e.activation(out=s0[i2], in_=pg0, func=AF.Sigmoid), reads=bg0, writes=[b_s0[i2]])
                S.op("dve", lambda e: e.tensor_tensor(out=m0all[:, d, :], in0=s0[i2], in1=pua, op=ALU.mult),
                     reads=[b_s0[i2]] + bua, writes=[b_m0[d]])

            for j in range(16):
                S.op("dve", lambda e, j=j: e.scalar_tensor_tensor(
                    out=y_f[:, j, :], in0=xsT[:, j, :], scalar=dfm[:, j:j + 1], in1=y_f[:, j, :],
                    op0=ALU.mult, op1=ALU.add), reads=[b_xs[j], b_const] + b_ych, writes=[b_y[j]])
                S.op("dve", lambda e, j=j: e.tensor_tensor(out=y_f[:, j, :], in0=y_f[:, j, :], in1=szT[:, j, :], op=ALU.mult),
                     reads=[b_y[j], b_szT[j]], writes=[b_y[j]])
                if j % 2 == 1:
                    merge1(j // 2)
            S.phase = "ssd_norm"
            nb = {}

            def n_sq(g):
                pb, bb = bank()
                nb[g] = (pb, bb)
                for eh in range(2):
                    j = g * 2 + eh
                    s_ = sqz[(g % 2) * 2 + eh]
                    bs_ = b_sqz[(g % 2) * 2 + eh]
                    S.op("act", lambda e, s_=s_, j=j: e.activation(out=s_, in_=y_f[:, j, :], func=AF.Square),
                         reads=[b_y[j]], writes=[bs_])
                    S.op("pe", lambda e, s_=s_, pb=pb, eh=eh: e.matmul(pb, lhsT=ones_bf[:], rhs=s_, start=(eh == 0), stop=(eh == 1)),
                         reads=[bs_, b_const], writes=bb)

            def n_rs(g):
                pb, bb = nb.pop(g)
                rg = rstd_g[g % 2]
                S.op("act", lambda e: e.activation(out=rg, in_=pb, func=AF.Ln, bias=float(256 * EPS)),
                     reads=bb, writes=[b_rg[g % 2]])
                S.op("act", lambda e: e.activation(out=rg, in_=rg, func=AF.Exp, scale=-0.5),
                     reads=[b_rg[g % 2]], writes=[b_rg[g % 2]])

            def n_sc(g):
                rg = rstd_g[g % 2]
                for eh in range(2):
                    j = g * 2 + eh
                    S.op("dve", lambda e, j=j: e.scalar_tensor_tensor(
                        out=xsT[:, j, :], in0=y_f[:, j, :], scalar=snw16[:, j:j + 1], in1=rg,
                        op0=ALU.mult, op1=ALU.mult), reads=[b_y[j], b_rg[g % 2], b_const], writes=[b_xs[j]])

            for it in range(8 + 2):
                if it < 8:
                    n_sq(it)
                if 0 <= it - 1 < 8:
                    n_rs(it - 1)
                if 0 <= it - 2 < 8:
                    n_sc(it - 2)
            yn, b_yn = xsT, b_xs
            if stage == "ssd":
                hsel = cfg.get("ssd_half", 0)
                for k in range(8):
                    S.op("dve", lambda e, k=k: e.tensor_copy(out=xT[:, k, :], in_=yn[:, hsel * 8 + k, :]),
                         reads=[b_yn[hsel * 8 + k], b_xT[k]], writes=[b_xT[k]])
                return
            S.phase = "merge"
            for d in range(8):
                wsb, bwb = w_acquire("mb%d" % d)
                wb = seg(wsb, 0, 24, 128)
                pg1, bg1 = bank()
                S.op("pe", mm_group(pg1, [(wb[:, k, :], hT[:, k, :]) for k in range(8)]), reads=[bwb] + b_hT, writes=bg1)
                pub, bub = bank()
                S.op("pe", mm_group(pub, [(wb[:, 8 + k, :], yn[:, k, :]) for k in range(16)]), reads=[bwb] + b_yn, writes=bub)
                i2 = d % 2
                S.op("act", lambda e, i2=i2, pg1=pg1: e.activation(out=s1[i2], in_=pg1, func=AF.Sigmoid),
                     reads=bg1, writes=[b_s1[i2]])
                S.op("dve", lambda e, i2=i2, pub=pub: e.tensor_tensor(out=m1[i2], in0=s1[i2], in1=pub, op=ALU.mult),
                     reads=[b_s1[i2]] + bub, writes=[b_m1[i2]])
                S.op(cfg.get("merge_add_eng", "dve"), lambda e, i2=i2, d=d: e.tensor_tensor(out=merged[:, d, :], in0=m0all[:, d, :], in1=m1[i2], op=ALU.add),
                     reads=[b_m0[d], b_m1[i2]], writes=[b_mg[d]])
            S.phase = "wout"
            preload_ln()
            for pw in range(2):
                wsl, bw = w_acquire("wo%d" % pw)
                wv = seg(wsl, 0, 8, 512)
                for cc in range(4):
                    d = pw * 4 + cc
                    pb, bb = bank()
                    S.op("pe", mm_group(pb, [(wv[:, k, cc * 128:(cc + 1) * 128], merged[:, k, :]) for k in range(8)]),
                         reads=[bw] + b_mg, writes=bb)
                    S.op("dve", lambda e, pb=pb, d=d: e.tensor_tensor(out=xT[:, d, :], in0=xT[:, d, :], in1=pb, op=ALU.add),
                         reads=bb + [b_xT[d]], writes=[b_xT[d]])
                    stats_push(xT[:, d, :], b_xT[d], last=(d == 7))

        tiles_per_seq = SEQ // T
        for ti in range(n_tiles):
            if ti == 0 or stage != "full":
                load_transposes()
            if ti + 1 < n_tiles and stage == "ffn1":
                load_x(ti + 1)
            rmsnorm_h(0)
            ffn("ffn1")
            if stage == "ffn1":
                store_out(ti, False)
                continue
            rmsnorm_h(1)
            S.epoch()
            mixer(ti % tiles_per_seq == 0)
            S.epoch(rearm=ffn_arena_bufs)
            xhooks = None
            if ti + 1 < n_tiles:
                xhooks = {1 + 2 * tb: (lambda tb=tb, ti=ti: load_x(ti + 1, [tb])) for tb in range(NT)}
                if stage != "full":
                    load_x(ti + 1)
            if stage in ("mix", "gla", "ssd", "ssd_conv", "ssd_rec"):
                store_out(ti, False)
                continue
            rmsnorm_h(2)
            ffn("ffn2", hooks=xhooks)
            store_and_load(ti, ti + 1 < n_tiles)
        S.wait_all_dma("pool")
        S.wait_all_dma("sp")
        with nc.Block() as block:
            S.emit(block)
    nc._arena_log = dict(A.log)
    nc._pe_labels = list(S.pe_labels)
    return nc


def make_cpack(inp):
    f = lambda a: np.asarray(a, dtype=np.float32)
    cols = []
    for n in ("ffn1_norm", "mix_norm", "ffn2_norm"):
        cols.append(f(inp[n]).reshape(8, 128).T)
    cols.append(f(inp["final_norm"]).reshape(8, 128).T)
    cols.append(f(inp["gla_norm"]).reshape(2, 128).T)
    cols.append(f(inp["ssd_norm"]).reshape(16, 128).T)
    cw = f(inp["ssd_conv_w"]).reshape(4, 32, 128)
    cols.append(cw.transpose(2, 1, 0).reshape(128, 128))
    cols.append(f(inp["ssd_conv_b"]).reshape(32, 128).T)
    dd = f(inp["ssd_d"]).reshape(32)
    cols.append(np.repeat(dd.reshape(16, 2), 64, axis=1).T)
    cols.append(np.broadcast_to(f(inp["ssd_dt_bias"]).reshape(1, 32), (128, 32)))
    cols.append(np.broadcast_to(f(inp["ssd_a_log"]).reshape(1, 32), (128, 32)))
    out = np.ascontiguousarray(np.concatenate(cols, axis=1), dtype=np.float32)
    assert out.shape == (128, CP_COLS)
    return out


def make_weight_map(inp):
    f = lambda a, s: np.ascontiguousarray(np.asarray(a, dtype=np.float32).reshape(s))
    m = {
        "ffn1_wg": f(inp["ffn1_w_gate"], (D, DFF)), "ffn1_wu": f(inp["ffn1_w_up"], (D, DFF)),
        "ffn1_wd": f(inp["ffn1_w_down"], (DFF, D)), "w_in": f(inp["w_in"], (D, IN_DIM)),
        "gla_wo": f(inp["gla_w_o"], (D, D)), "ssd_wo": f(inp["ssd_w_o"], (2 * D, D)),
        "w_out": f(inp["w_out"], (D, D)),
        "ffn2_wg": f(inp["ffn2_w_gate"], (D, DFF)), "ffn2_wu": f(inp["ffn2_w_up"], (D, DFF)),
        "ffn2_wd": f(inp["ffn2_w_down"], (DFF, D)),
    }
    m["wfu17"] = np.ascontiguousarray(np.concatenate(
        [np.asarray(inp["gla_w_f_up"], np.float32).reshape(16, 512),
         np.asarray(inp["gla_b_f"], np.float32).reshape(1, 512)], axis=0))
    m["cpack"] = make_cpack(inp)
    return m


_NC_CACHE = {}


def kernel(**inputs):
    x = np.asarray(inputs["x"], dtype=np.float32)
    bsz = x.shape[0]
    per = bsz // NCORES
    key = ("full", per)
    if key not in _NC_CACHE:
        _NC_CACHE[key] = build_nc(n_seq=per, stage="full")
    nc = _NC_CACHE[key]
    wm = make_weight_map(inputs)
    in_maps = []
    for c in range(NCORES):
        m = dict(wm)
        m["x"] = np.ascontiguousarray(x[c * per:(c + 1) * per].reshape(per * SEQ, D))
        in_maps.append(m)
    res = run_bass_kernel_spmd(nc, in_maps, core_ids=list(range(NCORES)))
    out = np.concatenate([np.asarray(r["out"], dtype=np.float32).reshape(per, SEQ, D) for r in res.results], axis=0)
    return out
```
